# Optimizing a Trainium2 kernel written in Bass

```python
import math
import jax, jax.numpy as jnp
from jax import lax
import numpy as np

D_MODEL = 2048
BATCH = 4
SEQ = 2048
DEPTH = 4
DEC_BATCH = 8
DEC_SEQ = 4
PAST_LEN = 16384
PAGE_SIZE = 128

N_EVEN = (DEPTH + 1) // 2
N_ODD = DEPTH // 2
MIX_W = D_MODEL
A_HEAD_DIM = 128
A_W = MIX_W // 2
A_HEADS = A_W // A_HEAD_DIM
MOBA_BLOCK = 256
MOBA_TOPK = 3
MOBA_Q_CHUNK = 16
B_HEAD_DIM = 64
B_W = MIX_W // 2
B_HEADS = B_W // B_HEAD_DIM
LORA_W = 64
LORA_A = 64
B_COLS = 3 * B_W + LORA_W + LORA_A
RWKV_GN_EPS = 64e-5
GATE_EVEN = A_W + B_W
COLS_EVEN = 3 * A_W + B_COLS + GATE_EVEN
C_W = MIX_W
POOL_WINDOWS = (2, 4, 8, 16)
POOL_GROUPS = len(POOL_WINDOWS)
POOL_GW = C_W // POOL_GROUPS
POOL_BUF = max(POOL_WINDOWS) - 1
COLS_ODD = 2 * C_W
MEM_LEN = 256
X_HEADS = 4
X_HEAD_DIM = D_MODEL // X_HEADS
T5_BUCKETS = 32
T5_MAX_DIST = 128
LN_EPS = 1e-5
ALPHA = (2 * DEPTH) ** 0.25
BETA = (8 * DEPTH) ** -0.25

kernel_name = 'moba_rwkv7_pool_hybrid_step'


def layer_norm(x, g, b):
    xf = x.astype(jnp.float32)
    mu = jnp.mean(xf, axis=-1, keepdims=True)
    var = jnp.mean(jnp.square(xf - mu), axis=-1, keepdims=True)
    y = (xf - mu) * lax.rsqrt(var + LN_EPS) * g.astype(jnp.float32) + b.astype(jnp.float32)
    return y.astype(x.dtype)


def t5_bucket(rel):
    exact = T5_BUCKETS // 2
    rel = jnp.maximum(rel, 0)
    relf = jnp.maximum(rel, exact).astype(jnp.float32)
    large = exact + (jnp.log(relf / exact) / math.log(T5_MAX_DIST / exact) * (T5_BUCKETS - exact)).astype(jnp.int32)
    large = jnp.minimum(large, T5_BUCKETS - 1)
    return jnp.where(rel < exact, rel, large)


def moba_attend(q, k_all, v_all, pos0, t5_table):
    Bq, Tq, H, D = q.shape
    Lk = k_all.shape[1]
    nb = -(-Lk // MOBA_BLOCK)
    pad = nb * MOBA_BLOCK - Lk
    kb = jnp.pad(k_all, ((0, 0), (0, pad), (0, 0), (0, 0))).reshape(Bq, nb, MOBA_BLOCK, H, D).transpose(0, 3, 1, 2, 4)
    vb = jnp.pad(v_all, ((0, 0), (0, pad), (0, 0), (0, 0))).reshape(Bq, nb, MOBA_BLOCK, H, D).transpose(0, 3, 1, 2, 4)
    kmean = jnp.mean(kb.astype(jnp.float32), axis=3)
    k_sel = min(MOBA_TOPK, nb)
    qc = math.gcd(Tq, MOBA_Q_CHUNK)
    n_chunks = Tq // qc
    qh = jnp.moveaxis(q.transpose(0, 2, 1, 3).reshape(Bq, H, n_chunks, qc, D), 2, 0)
    bi = jnp.arange(Bq)[:, None, None, None]
    hi = jnp.arange(H)[None, :, None, None]
    offs = jnp.arange(MOBA_BLOCK, dtype=jnp.int32)
    scale = 1.0 / math.sqrt(D)
    bias_tab = t5_table.T.astype(jnp.float32)

    def chunk(args):
        c, qq = args
        pos = pos0 + c * qc + jnp.arange(qc, dtype=jnp.int32)
        own = pos // MOBA_BLOCK
        gate = jnp.einsum('bhqd,bhnd->bhqn', qq.astype(jnp.float32), kmean)
        past = jnp.arange(nb)[None, :] < own[:, None]
        gate = jnp.where(past, gate, -jnp.inf)
        _, top = lax.top_k(gate, k_sel)
        top_ok = top < own[:, None]
        own_b = jnp.broadcast_to(own[:, None], (Bq, H, qc, 1)).astype(top.dtype)
        idx = jnp.concatenate([top, own_b], axis=-1)
        ok = jnp.concatenate([top_ok, jnp.ones_like(own_b, dtype=bool)], axis=-1)
        kg = kb[bi, hi, idx]
        vg = vb[bi, hi, idx]
        keypos = idx[..., None] * MOBA_BLOCK + offs
        rel = pos[:, None, None] - keypos
        logits = jnp.einsum('bhqd,bhqjkd->bhqjk', qq, kg).astype(jnp.float32) * scale
        logits = logits + bias_tab[hi[..., None], t5_bucket(rel)]
        mask = ok[..., None] & (rel >= 0)
        logits = jnp.where(mask, logits, -1e30)
        p = jax.nn.softmax(logits.reshape(Bq, H, qc, -1), axis=-1).reshape(logits.shape)
        return jnp.einsum('bhqjk,bhqjkd->bhqd', p.astype(vg.dtype), vg)

    out = lax.map(chunk, (jnp.arange(n_chunks, dtype=jnp.int32), qh))
    out = jnp.moveaxis(out, 0, 2).reshape(Bq, H, Tq, D).transpose(0, 2, 1, 3)
    return out.reshape(Bq, Tq, H * D)


def rwkv_scan(r, decay, kk, a, k, v, S0):
    def step(S, inp):
        r_t, w_t, kk_t, a_t, k_t, v_t = inp
        sa = jnp.einsum('bhvk,bhk->bhv', S, -kk_t)
        S = S * w_t[:, :, None, :] + sa[..., None] * (kk_t * a_t)[:, :, None, :] + v_t[..., None] * k_t[:, :, None, :]
        return S, jnp.einsum('bhvk,bhk->bhv', S, r_t)
    seq = tuple(jnp.moveaxis(t, 1, 0) for t in (r, decay, kk, a, k, v))
    S_T, ys = lax.scan(step, S0.astype(jnp.float32), seq)
    return jnp.moveaxis(ys, 0, 1), S_T


def rwkv_mix(xb, shift_prev, S0, mu, w0, w_up, a0, a_up, k_k, k_a, r_k, gn_g, gn_b):
    Bq, T, _ = xb.shape
    prev = jnp.concatenate([shift_prev[:, None, :].astype(xb.dtype), xb[:, :-1]], axis=1)
    xm = (xb + (prev - xb) * mu).astype(jnp.float32)
    r, k, v, wd, ad = jnp.split(xm, [B_W, 2 * B_W, 3 * B_W, 3 * B_W + LORA_W], axis=-1)
    w_log = -jnp.exp(-jax.nn.softplus(-(w0 + jnp.tanh(wd) @ w_up)) - 0.5)
    a = jax.nn.sigmoid(a0 + ad @ a_up)
    hs = lambda t: t.reshape(Bq, T, B_HEADS, B_HEAD_DIM)
    kk = hs(k * k_k)
    kk = kk / jnp.maximum(jnp.sqrt(jnp.sum(jnp.square(kk), axis=-1, keepdims=True)), 1e-12)
    k = k * (1.0 + (a - 1.0) * k_a)
    r, k, v, a, decay = hs(r), hs(k), hs(v), hs(a), hs(jnp.exp(w_log))
    y, S_T = rwkv_scan(r, decay, kk, a, k, v, S0)
    m = jnp.mean(y, axis=-1, keepdims=True)
    var = jnp.mean(jnp.square(y - m), axis=-1, keepdims=True)
    y = ((y - m) * lax.rsqrt(var + RWKV_GN_EPS)).reshape(Bq, T, B_W) * gn_g + gn_b
    y = y + (jnp.sum(r * k * r_k, axis=-1, keepdims=True) * v).reshape(Bq, T, B_W)
    return y.astype(xb.dtype), xb[:, -1], S_T.astype(xb.dtype)


def even_mixer(x, past_k, past_v, shift_prev, S0, pos0, w_in, w_out, t5_table, rw):
    Bq, T, _ = x.shape
    h = x @ w_in
    q, k, v, xb, z = jnp.split(h, [A_W, 2 * A_W, 3 * A_W, 3 * A_W + B_COLS], axis=-1)
    q = q.reshape(Bq, T, A_HEADS, A_HEAD_DIM)
    k = k.reshape(Bq, T, A_HEADS, A_HEAD_DIM)
    v = v.reshape(Bq, T, A_HEADS, A_HEAD_DIM)
    if past_k is None:
        k_all, v_all = k, v
    else:
        k_all = jnp.concatenate([past_k.astype(k.dtype), k], axis=1)
        v_all = jnp.concatenate([past_v.astype(v.dtype), v], axis=1)
    a_out = moba_attend(q, k_all, v_all, pos0, t5_table)
    b_out, shift_new, S_new = rwkv_mix(xb, shift_prev, S0, *rw)
    y = jnp.concatenate([a_out.astype(x.dtype), b_out], axis=-1) * jax.nn.silu(z)
    return y @ w_out, k, v, shift_new, S_new


def pool_mix(xc, buf, pos0):
    Bq, T, C = xc.shape
    z = jnp.concatenate([buf.astype(jnp.float32), xc.astype(jnp.float32)], axis=1)
    cs = jnp.concatenate([jnp.zeros((Bq, 1, C), jnp.float32), jnp.cumsum(z, axis=1)], axis=1)
    pos = pos0 + jnp.arange(T, dtype=jnp.int32)
    outs = []
    for g, w in enumerate(POOL_WINDOWS):
        sl = slice(g * POOL_GW, (g + 1) * POOL_GW)
        s = cs[:, POOL_BUF + 1:POOL_BUF + T + 1, sl] - cs[:, POOL_BUF + 1 - w:POOL_BUF + T + 1 - w, sl]
        cnt = jnp.minimum(w, pos + 1).astype(jnp.float32)
        outs.append(s / cnt[None, :, None])
    mean = jnp.concatenate(outs, axis=-1)
    return (mean - xc.astype(jnp.float32)).astype(xc.dtype), z[:, -POOL_BUF:].astype(xc.dtype)


def odd_mixer(x, buf, pos0, w_in, group_w, scale, w_out):
    Bq, T, _ = x.shape
    h = x @ w_in
    xc, z = h[..., :C_W], h[..., C_W:]
    pooled, new_buf = pool_mix(xc, buf, pos0)
    y = jnp.einsum('btgc,gcd->btgd', pooled.reshape(Bq, T, POOL_GROUPS, POOL_GW), group_w).reshape(Bq, T, C_W) * scale
    return (y * jax.nn.silu(z)) @ w_out, new_buf


def cross_attn(x, mk, mv, w_q, w_o):
    Bq, T, _ = x.shape
    q = (x @ w_q).reshape(Bq, T, X_HEADS, X_HEAD_DIM)
    s = jnp.einsum('bthd,bmhd->bhtm', q, mk.astype(q.dtype)).astype(jnp.float32) / math.sqrt(X_HEAD_DIM)
    p = jax.nn.softmax(s, axis=-1).astype(x.dtype)
    o = jnp.einsum('bhtm,bmhd->bthd', p, mv.astype(x.dtype)).reshape(Bq, T, X_HEADS * X_HEAD_DIM)
    return o @ w_o


def setup_inputs(seed: int = 0) -> dict:
    key = jax.random.key(seed)
    keys = jax.random.split(key, 48)
    ctr = [0]

    def nk():
        k = keys[ctr[0]]
        ctr[0] += 1
        return k

    def nrm(shape, s):
        return jax.random.normal(nk(), shape, jnp.float32) * s

    def uni(shape):
        return jax.random.uniform(nk(), shape, jnp.float32)

    n_pages = PAST_LEN // PAGE_SIZE
    n_phys = (DEC_BATCH * n_pages * 5) // 4
    perm = jax.random.permutation(nk(), n_phys)
    page_table = perm[:DEC_BATCH * n_pages].reshape(DEC_BATCH, n_pages).astype(jnp.int32)
    return {
        'x_prompt': nrm((BATCH, SEQ, D_MODEL), 1.0),
        'x_sample': nrm((DEC_BATCH, DEC_SEQ, D_MODEL), 1.0),
        'cache_moba_k': nrm((N_EVEN, n_phys, PAGE_SIZE, A_HEADS, A_HEAD_DIM), 1.0),
        'cache_moba_v': nrm((N_EVEN, n_phys, PAGE_SIZE, A_HEADS, A_HEAD_DIM), 1.0),
        'page_table': page_table,
        'state_rwkv': nrm((N_EVEN, DEC_BATCH, B_HEADS, B_HEAD_DIM, B_HEAD_DIM), 0.1),
        'state_shift': nrm((N_EVEN, DEC_BATCH, B_COLS), 1.0),
        'state_pool': nrm((N_ODD, DEC_BATCH, POOL_BUF, C_W), 1.0),
        'cache_mem_k': nrm((DEPTH, DEC_BATCH, MEM_LEN, X_HEADS, X_HEAD_DIM), 1.0),
        'cache_mem_v': nrm((DEPTH, DEC_BATCH, MEM_LEN, X_HEADS, X_HEAD_DIM), 1.0),
        'mem_prompt': nrm((BATCH, MEM_LEN, D_MODEL), 1.0),
        'w_in_even': nrm((N_EVEN, D_MODEL, COLS_EVEN), D_MODEL ** -0.5),
        'w_out_even': nrm((N_EVEN, GATE_EVEN, D_MODEL), BETA * GATE_EVEN ** -0.5),
        'rwkv_mu': uni((N_EVEN, B_COLS)),
        'rwkv_w0': nrm((N_EVEN, B_W), 0.5),
        'rwkv_w_up': nrm((N_EVEN, LORA_W, B_W), 0.5 * LORA_W ** -0.5),
        'rwkv_a0': nrm((N_EVEN, B_W), 0.5),
        'rwkv_a_up': nrm((N_EVEN, LORA_A, B_W), 0.5 * LORA_A ** -0.5),
        'rwkv_k_k': 0.85 + nrm((N_EVEN, B_W), 0.05),
        'rwkv_k_a': 1.0 + nrm((N_EVEN, B_W), 0.05),
        'rwkv_r_k': nrm((N_EVEN, B_HEADS, B_HEAD_DIM), 0.1),
        'rwkv_gn_g': 1.0 + nrm((N_EVEN, B_W), 0.05),
        'rwkv_gn_b': nrm((N_EVEN, B_W), 0.02),
        't5_bias': nrm((T5_BUCKETS, A_HEADS), 0.5),
        'w_in_odd': nrm((N_ODD, D_MODEL, COLS_ODD), D_MODEL ** -0.5),
        'pool_w': nrm((N_ODD, POOL_GROUPS, POOL_GW, POOL_GW), POOL_GW ** -0.5),
        'pool_scale': 1.0 + nrm((N_ODD, C_W), 0.05),
        'w_out_odd': nrm((N_ODD, C_W, D_MODEL), BETA * C_W ** -0.5),
        'xattn_w_q': nrm((DEPTH, D_MODEL, X_HEADS * X_HEAD_DIM), D_MODEL ** -0.5),
        'xattn_w_k': nrm((DEPTH, D_MODEL, X_HEADS * X_HEAD_DIM), D_MODEL ** -0.5),
        'xattn_w_v': nrm((DEPTH, D_MODEL, X_HEADS * X_HEAD_DIM), D_MODEL ** -0.5),
        'xattn_w_o': nrm((DEPTH, X_HEADS * X_HEAD_DIM, D_MODEL), BETA * D_MODEL ** -0.5),
        'ln_mix_g': 1.0 + nrm((DEPTH, D_MODEL), 0.05),
        'ln_mix_b': nrm((DEPTH, D_MODEL), 0.02),
        'ln_x_g': 1.0 + nrm((DEPTH, D_MODEL), 0.05),
        'ln_x_b': nrm((DEPTH, D_MODEL), 0.02),
    }


def reference(x_prompt, x_sample, cache_moba_k, cache_moba_v, page_table, state_rwkv, state_shift, state_pool,
              cache_mem_k, cache_mem_v, mem_prompt, w_in_even, w_out_even, rwkv_mu, rwkv_w0, rwkv_w_up, rwkv_a0,
              rwkv_a_up, rwkv_k_k, rwkv_k_a, rwkv_r_k, rwkv_gn_g, rwkv_gn_b, t5_bias, w_in_odd, pool_w, pool_scale,
              w_out_odd, xattn_w_q, xattn_w_k, xattn_w_v, xattn_w_o, ln_mix_g, ln_mix_b, ln_x_g, ln_x_b):
    xp, xs = x_prompt, x_sample
    bp, bs = xp.shape[0], xs.shape[0]
    kp_l, vp_l, sp_l, shp_l, poolp_l, mkp_l, mvp_l = [], [], [], [], [], [], []
    ks_l, vs_l, ss_l, shs_l, pools_l = [], [], [], [], []
    for l in range(DEPTH):
        if l % 2 == 0:
            e = l // 2
            rw = (rwkv_mu[e], rwkv_w0[e], rwkv_w_up[e], rwkv_a0[e], rwkv_a_up[e], rwkv_k_k[e], rwkv_k_a[e],
                  rwkv_r_k[e], rwkv_gn_g[e], rwkv_gn_b[e])
            mp, kp, vp, shp, Sp = even_mixer(
                xp, None, None, jnp.zeros((bp, B_COLS), xp.dtype),
                jnp.zeros((bp, B_HEADS, B_HEAD_DIM, B_HEAD_DIM), jnp.float32),
                0, w_in_even[e], w_out_even[e], t5_bias, rw)
            past_k = cache_moba_k[e][page_table].reshape(bs, -1, A_HEADS, A_HEAD_DIM)
            past_v = cache_moba_v[e][page_table].reshape(bs, -1, A_HEADS, A_HEAD_DIM)
            ms, ks_, vs_, shs, Ss = even_mixer(
                xs, past_k, past_v, state_shift[e], state_rwkv[e], PAST_LEN,
                w_in_even[e], w_out_even[e], t5_bias, rw)
            kp_l.append(kp); vp_l.append(vp); sp_l.append(Sp); shp_l.append(shp)
            ks_l.append(ks_); vs_l.append(vs_); ss_l.append(Ss); shs_l.append(shs)
        else:
            o = l // 2
            mp, bufp = odd_mixer(xp, jnp.zeros((bp, POOL_BUF, C_W), xp.dtype), 0,
                                 w_in_odd[o], pool_w[o], pool_scale[o], w_out_odd[o])
            ms, bufs = odd_mixer(xs, state_pool[o], PAST_LEN,
                                 w_in_odd[o], pool_w[o], pool_scale[o], w_out_odd[o])
            poolp_l.append(bufp); pools_l.append(bufs)
        xp = layer_norm(ALPHA * xp + mp, ln_mix_g[l], ln_mix_b[l])
        xs = layer_norm(ALPHA * xs + ms, ln_mix_g[l], ln_mix_b[l])
        mk_p = (mem_prompt @ xattn_w_k[l]).reshape(bp, MEM_LEN, X_HEADS, X_HEAD_DIM)
        mv_p = (mem_prompt @ xattn_w_v[l]).reshape(bp, MEM_LEN, X_HEADS, X_HEAD_DIM)
        mkp_l.append(mk_p); mvp_l.append(mv_p)
        xp = layer_norm(ALPHA * xp + cross_attn(xp, mk_p, mv_p, xattn_w_q[l], xattn_w_o[l]), ln_x_g[l], ln_x_b[l])
        xs = layer_norm(ALPHA * xs + cross_attn(xs, cache_mem_k[l], cache_mem_v[l], xattn_w_q[l], xattn_w_o[l]),
                        ln_x_g[l], ln_x_b[l])
    new_moba_k_prompt = jnp.stack(kp_l)
    new_moba_v_prompt = jnp.stack(vp_l)
    new_rwkv_prompt = jnp.stack(sp_l)
    new_shift_prompt = jnp.stack(shp_l)
    new_pool_prompt = jnp.stack(poolp_l)
    new_mem_k_prompt = jnp.stack(mkp_l)
    new_mem_v_prompt = jnp.stack(mvp_l)
    new_moba_k_sample = jnp.stack(ks_l)
    new_moba_v_sample = jnp.stack(vs_l)
    new_rwkv_sample = jnp.stack(ss_l)
    new_shift_sample = jnp.stack(shs_l)
    new_pool_sample = jnp.stack(pools_l)
    return (xp, xs, new_moba_k_prompt, new_moba_v_prompt, new_rwkv_prompt, new_shift_prompt, new_pool_prompt,
            new_mem_k_prompt, new_mem_v_prompt, new_moba_k_sample, new_moba_v_sample, new_rwkv_sample,
            new_shift_sample, new_pool_sample)
```

```python
import numpy as np
from contextlib import ExitStack
import concourse.bass as bass
import concourse.mybir as mybir

F32 = mybir.dt.float32
BF16 = mybir.dt.bfloat16
I32 = mybir.dt.int32
AF = mybir.ActivationFunctionType
ALU = mybir.AluOpType
AX = mybir.AxisListType


class Buf:
    __slots__ = ("t", "name", "w", "r", "dram", "par", "sem")

    def __init__(self, t, name, dram=False):
        self.t = t
        self.name = name
        self.w = []
        self.r = {}
        self.dram = dram
        self.par = None
        self.sem = None

    def __getitem__(self, k):
        return self.t[k]


class FW:
    EPOCH = 12000

    def __init__(self, nc):
        self.nc = nc
        self.es = ExitStack()
        self.perm = ExitStack()
        self.eng = {"pe": nc.tensor, "act": nc.scalar, "dve": nc.vector, "pool": nc.gpsimd, "sp": nc.sync}
        self.sem = {}
        self.cnt = {}
        self.nsem = 0
        for e in self.eng:
            self._new_epoch(e)
        self.waited = {e: {} for e in self.eng}
        self.dpool = []
        self.dcnt = []
        self.dlast = []
        for i in range(48):
            s = self.perm.enter_context(nc.semaphore("dq%d" % i))
            self.dpool.append(s)
            self.dcnt.append(0)
            self.dlast.append(None)
        self.dnext = 0
        self.ninstr = 0
        self.out_tickets = []

    def _new_epoch(self, e):
        s = self.perm.enter_context(self.nc.semaphore("e_%s_%d" % (e, self.nsem)))
        self.nsem += 1
        self.sem[e] = s
        self.cnt[e] = 0

    def sb(self, name, shape, dt=F32):
        self.nsem += 1
        name = "%s_u%d" % (name, self.nsem)
        t = self.es.enter_context(self.nc.sbuf_tensor(name, list(shape), dt))
        return Buf(t, name)

    def ps(self, name, shape, dt=F32):
        t = self.es.enter_context(self.nc.psum_tensor(name, list(shape), dt))
        return Buf(t, name)

    def dram(self, name, shape, dt=F32, kind="Internal"):
        t = self.nc.dram_tensor(name, list(shape), dt, kind=kind)
        return Buf(t, name, dram=True)

    def _wait(self, e, tk):
        if tk is None:
            return
        sem, val, src = tk
        if src == e and e in ("pe", "sp"):
            return
        k = id(sem)
        if self.waited[e].get(k, -1) >= val:
            return
        self.eng[e].wait_ge(sem, val)
        self.waited[e][k] = val

    def _deps(self, e, reads, writes, par=False):
        reads = [b.par or b for b in reads]
        writes = [b.par or b for b in writes]
        for b in reads:
            for tk in b.w:
                self._wait(e, tk)
        for b in writes:
            if not par:
                for tk in b.w:
                    self._wait(e, tk)
            for tk in b.r.values():
                self._wait(e, tk)

    def _record(self, tk, reads, writes, par=False):
        reads = [b.par or b for b in reads]
        writes = [b.par or b for b in writes]
        for b in writes:
            if par:
                b.w.append(tk)
            else:
                b.w = [tk]
            b.r = {}
        for b in reads:
            if b in writes:
                continue
            b.r[id(tk[0])] = tk

    def op(self, e, fn, reads=(), writes=(), accum=False):
        self._deps(e, reads, writes)
        ins = fn(self.eng[e])
        if self.cnt[e] >= self.EPOCH:
            self._new_epoch(e)
        self.cnt[e] += 1
        ins.then_inc(self.sem[e], 1)
        tk = (self.sem[e], self.cnt[e], e)
        self._record(tk, reads, writes)
        self.ninstr += 1
        return tk

    def dma(self, out_ap, in_ap, reads=(), writes=(), q="sp", is_out=False, indirect=None, par=False):
        i = self.dnext
        self.dnext = (self.dnext + 1) % len(self.dpool)
        self._wait(q, self.dlast[i])
        self._deps(q, reads, writes, par)
        if indirect is not None:
            ins = self.eng[q].indirect_dma_start(out=out_ap, out_offset=None, in_=in_ap, in_offset=indirect)
        else:
            ins = self.eng[q].dma_start(out=out_ap, in_=in_ap)
        self.dcnt[i] += 16
        ins.then_inc(self.dpool[i], 16)
        tk = (self.dpool[i], self.dcnt[i], "dma")
        self.dlast[i] = tk
        self._record(tk, reads, writes, par)
        if is_out:
            self.out_tickets.append(tk)
        self.ninstr += 1
        return tk

    def swdma_gather(self, buf, in_ap, idx_ap, idxbuf, out_ap=None, element_offset=0):
        self.nsem += 1
        sem = self.perm.enter_context(self.nc.semaphore("sw%d" % self.nsem))
        self._deps("pool", [idxbuf], [buf])
        ins = self.eng["pool"].indirect_dma_start(out=(out_ap if out_ap is not None else buf[:]), out_offset=None, in_=in_ap,
                                                  in_offset=bass.IndirectOffsetOnAxis(ap=idx_ap, axis=0), element_offset=element_offset)
        ins.then_inc(sem, 16)
        tk = (sem, 16, "dma")
        self._record(tk, [idxbuf], [buf])
        self.ninstr += 1
        return tk

    def finish(self):
        for tk in self.dlast:
            self._wait("sp", tk)
        for e in ("pe", "act", "dve", "pool"):
            if self.cnt[e] > 0:
                self._wait("sp", (self.sem[e], self.cnt[e], e))

    def close(self):
        self.es.close()
        self.perm.close()


def _fw_scope(self):
    from contextlib import contextmanager

    @contextmanager
    def cm():
        old = self.es
        self.es = ExitStack()
        try:
            yield
        finally:
            self.barrier()
            self.es.close()
            self.es = old
    return cm()


def _fw_barrier(self):
    tks = []
    for e in ("pe", "act", "dve", "pool"):
        if self.cnt[e] > 0:
            tks.append((self.sem[e], self.cnt[e], e))
    for tk in self.dlast:
        if tk is not None:
            tks.append(tk)
    for e in self.eng:
        for tk in tks:
            if tk[2] == e and e != "dma":
                if e in ("pe", "sp"):
                    continue
            self._wait(e, tk)


FW.scope = _fw_scope
FW.barrier = _fw_barrier


import math


class Cfg:
    def __init__(s, D=2048, T=2048, NS=4, PAST=16384, ML=256, NPHYS=1280, DEPTH=4):
        s.D = D; s.T = T; s.NS = NS; s.PAST = PAST; s.ML = ML; s.NPHYS = NPHYS; s.L = DEPTH
        s.AW = D // 2; s.AH = s.AW // 128; s.BW = D // 2; s.BH = s.BW // 64
        s.BCOLS = 3 * s.BW + 128; s.COLS_E = 3 * s.AW + s.BCOLS + D; s.COLS_O = 2 * D
        s.XH = 4; s.XHD = D // 4; s.PGW = D // 4
        s.NT = T // 128; s.KC = D // 128; s.NPG = PAST // 128; s.NB = PAST // 256
        s.NE = (DEPTH + 1) // 2; s.NO = DEPTH // 2
        s.R = T + 128
        s.HR = T + 2 + 128
        s.ALPHA = (2 * DEPTH) ** 0.25


def t5_bucket_np(rel):
    rel = np.maximum(rel, 0)
    relf = np.maximum(rel, 16).astype(np.float32)
    large = 16 + (np.log(relf / np.float32(16)) / np.float32(math.log(128 / 16)) * np.float32(16)).astype(np.int32)
    large = np.minimum(large, 31)
    return np.where(rel < 16, rel, large)


def host_consts(c):
    k = {}
    k["ident"] = np.eye(128, dtype=np.float32)
    E = np.zeros((32, 384), np.float32)
    pad = np.zeros((128, 384), np.float32)
    for m in range(256):
        E[int(t5_bucket_np(np.array(255 - m))), m] = 1.0
    pad[:, 256:] = -1e30
    k["t5E"] = E
    k["t5pad"] = pad
    k["iota_p"] = np.arange(128, dtype=np.float32).reshape(128, 1)
    s_ = np.arange(128)[:, None]; t_ = np.arange(128)[None, :]
    same = (s_ // 64) == (t_ // 64)
    up_incl = (same & (s_ <= t_)).astype(np.float32)
    up_strict = (same & (s_ < t_)).astype(np.float32)
    lo_strict = (same & (s_ > t_)).astype(np.float32)
    k["rw_mask"] = np.concatenate([up_incl, up_strict, up_incl, up_strict, lo_strict], axis=1)
    k["tri_incl"] = up_incl
    k["tri_suf"] = lo_strict
    chunk_ind = np.zeros((128, 2), np.float32); chunk_ind[:64, 0] = 1; chunk_ind[64:, 1] = 1
    k["chunk_ind"] = chunk_ind
    wins = (2, 4, 8, 16)
    pm0 = np.zeros((4, 128, 128), np.float32); pmg = np.zeros((4, 128, 128), np.float32); pmp = np.zeros((4, 128, 128), np.float32)
    for g, w in enumerate(wins):
        for t in range(128):
            for s in range(max(0, t - w + 1), t + 1):
                pmg[g, s, t] = 1.0 / w
                pm0[g, s, t] = 1.0 / min(w, t + 1)
            pmg[g, t, t] -= 1.0
            pm0[g, t, t] -= 1.0
            for s in range(t - w + 1, 0):
                pmp[g, 128 + s, t] = 1.0 / w
    k["pm0"] = pm0.transpose(1, 0, 2).copy()
    k["pmg"] = pmg.transpose(1, 0, 2).copy()
    k["pmp"] = pmp.transpose(1, 0, 2).copy()
    return k


def build(c, debug_outs=()):
    nc = bass.Bass("TRN2", target_bir_lowering=False)
    fw = FW(nc)
    D, T, NS, KC, NT, AW, BW, AH, BH = c.D, c.T, c.NS, c.KC, c.NT, c.AW, c.BW, c.AH, c.BH
    NB512 = D // 512
    ALPHA = c.ALPHA

    def din(name, shape, dt=F32):
        return fw.dram(name, shape, dt, kind="ExternalInput")

    def dout(name, shape):
        return fw.dram(name, shape, F32, kind="ExternalOutput")

    def dscr(name, shape, dt=F32):
        return fw.dram(name, shape, dt, kind=("ExternalOutput" if name in debug_outs else "Internal"))

    I = {}
    for name, shape in [
        ("xp", [T, D]), ("xs", [NS, D]), ("ck", [c.NE, c.NPHYS * 128, AW]), ("cv", [c.NE, c.NPHYS * 128, AW]),
        ("srw", [c.NE, BH * 64, 64]), ("ssh", [c.NE, c.BCOLS]), ("spool", [c.NO, 15, D]),
        ("cmk", [c.L, c.ML, D]), ("cmv", [c.L, c.ML, D]), ("memp", [c.ML, D]),
        ("w_in_e", [c.NE, D, c.COLS_E]), ("w_out_e", [c.NE, D, D]), ("mu", [c.NE, c.BCOLS]),
        ("w0", [c.NE, BW]), ("w_up", [c.NE, 64, BW]), ("a0", [c.NE, BW]), ("a_up", [c.NE, 64, BW]),
        ("k_k", [c.NE, BW]), ("k_a", [c.NE, BW]), ("r_k", [c.NE, BW]), ("gn_g", [c.NE, BW]), ("gn_b", [c.NE, BW]),
        ("t5", [32, AH]), ("w_in_o", [c.NO, D, 2 * D]), ("pool_w", [c.NO, 4, c.PGW, c.PGW]), ("pool_scale", [c.NO, D]),
        ("w_out_o", [c.NO, D, D]), ("wq", [c.L, D, D]), ("wk", [c.L, D, D]), ("wv", [c.L, D, D]), ("wo", [c.L, D, D]),
        ("lnm_g", [c.L, D]), ("lnm_b", [c.L, D]), ("lnx_g", [c.L, D]), ("lnx_b", [c.L, D]),
        ("ident", [128, 128]), ("t5E", [32, 384]), ("t5pad", [128, 384]), ("iota_p", [128, 1]),
        ("rw_mask", [128, 640]), ("tri_incl", [128, 128]), ("tri_suf", [128, 128]), ("chunk_ind", [128, 2]),
        ("pm0", [128, 4, 128]), ("pmg", [128, 4, 128]), ("pmp", [128, 4, 128]),
    ]:
        I[name] = din(name, shape)
    I["ptab"] = din("ptab", [1, c.NPG], I32)
    O = {}
    for name, shape in [
        ("yp", [T, D]), ("ys", [NS, D]), ("nkp", [c.NE, T, AW]), ("nvp", [c.NE, T, AW]), ("nrp", [c.NE, BH * 64, 64]),
        ("nshp", [c.NE, c.BCOLS]), ("npp", [c.NO, 15, D]), ("nmk", [c.L, c.ML, D]), ("nmv", [c.L, c.ML, D]),
        ("nks", [c.NE, NS, AW]), ("nvs", [c.NE, NS, AW]), ("nrs", [c.NE, BH * 64, 64]), ("nshs", [c.NE, c.BCOLS]),
        ("nps", [c.NO, 15, D]),
    ]:
        O[name] = dout(name, shape)
    XR = dscr("XR", [c.R, D])
    HE = dscr("HE", [c.HR, c.COLS_E])
    HO = dscr("HO", [c.R, 2 * D])
    Y = dscr("Y", [c.R, D])
    TZ = dscr("TZ", [AH, 128, 384])
    OT = dscr("OT", [NT + 1, 128, KC * 128], BF16)

    import os as _os
    _stop = _os.environ.get('KSTOP', '')

    def chk(tag):
        return tag in _stop.split(',')

    def V(fn, r=(), w=()):
        return fw.op("dve", fn, r, w)

    def A(fn, r=(), w=()):
        return fw.op("act", fn, r, w)

    def G(fn, r=(), w=()):
        return fw.op("pool", fn, r, w)

    def P(fn, r=(), w=()):
        return fw.op("pe", fn, r, w)

    ident = fw.sb("ident_s", [128, 128])
    fw.dma(ident[:], I["ident"][:, :], [I["ident"]], [ident])
    zero = fw.sb("zero_s", [128, 512])
    V(lambda e: e.memset(zero[:], 0.0), [], [zero])
    pA = [fw.ps("pA%d" % i, [128, 512]) for i in range(4)]
    pT = [fw.ps("pT%d" % i, [128, 512]) for i in range(2)]
    pB = [fw.ps("pB%d" % i, [128, 512]) for i in range(2)]
    rr = {"ev": 0}

    def evac(out_ap, in_ap, r, w):
        rr["ev"] ^= 1
        if rr["ev"]:
            return A(lambda e: e.copy(out_ap, in_ap), r, w)
        return V(lambda e: e.tensor_copy(out_ap, in_ap), r, w)

    def herow(ti):
        return 1 + 128 * ti if ti < NT else T + 2

    def xrow(ti):
        return 128 * ti

    fw.dma(XR[0:T, :], I["xp"][:, :], [I["xp"]], [XR])
    fw.dma(XR[T:T + NS, :], I["xs"][:, :], [I["xs"]], [XR], par=True)
    for j in range(0, D, 512):
        fw.dma(XR[T + NS:c.R, j:j + 512], zero[0:128 - NS, :], [zero], [XR], par=True)
    for j in range(0, c.COLS_E, 512):
        n = min(512, c.COLS_E - j)
        fw.dma(HE[0:1, j:j + n], zero[0:1, 0:n], [zero], [HE], par=True)
    fw.barrier()

    def gemm_f1(w_ap, N, Hout, rowfn):
        with fw.scope():
            XT = fw.sb("XT", [128, KC, (NT + 1) * 128], BF16)
            xr_t = [fw.sb("f1xr%d" % i, [128, D]) for i in range(2)]
            for ti in range(NT + 1):
                xt = xr_t[ti % 2]
                fw.dma(xt[:], XR[xrow(ti):xrow(ti) + 128, :], [XR], [xt])
                for g in range(KC // 4):
                    pt = pT[g % 2]
                    for j in range(4):
                        kc = 4 * g + j
                        P(lambda e, pt=pt, j=j, kc=kc, xt=xt: e.transpose(pt[:, j * 128:(j + 1) * 128], xt[:, kc * 128:(kc + 1) * 128], ident[:]),
                          [xt, ident], [pt])
                    evac(XT[:, 4 * g:4 * g + 4, ti * 128:(ti + 1) * 128], pt[:].rearrange("p (a b) -> p a b", a=4), [pt], [XT])
            wts = [fw.sb("f1w%d" % i, [128, KC, 512], BF16) for i in range(2)]
            wst = fw.sb("f1wst", [128, KC, 512])
            stg = [fw.sb("f1s%d" % i, [128, 512]) for i in range(3)]
            wv = w_ap.rearrange("(kc p) n -> p kc n", p=128)
            cnt = 0
            for gi, n0 in enumerate(range(0, N, 512)):
                nn = min(512, N - n0)
                wt = wts[gi % 2]
                fw.dma(wst[:, :, 0:nn], wv[:, :, n0:n0 + nn], [], [wst])
                G(lambda e, wt=wt, nn=nn: e.tensor_copy(wt[:, :, 0:nn], wst[:, :, 0:nn]), [wst], [wt])
                for ti in range(NT + 1):
                    ps = pA[cnt % 4]
                    st = stg[cnt % 3]
                    cnt += 1
                    for kc in range(KC):
                        P(lambda e, ps=ps, kc=kc, wt=wt, ti=ti: e.matmul(ps[:, 0:nn], XT[:, kc, ti * 128:(ti + 1) * 128], wt[:, kc, 0:nn], start=(kc == 0), stop=(kc == KC - 1)),
                          [XT, wt], [ps])
                    evac(st[:, 0:nn], ps[:, 0:nn], [ps], [st])
                    r0 = rowfn(ti)
                    fw.dma(Hout[r0:r0 + 128, n0:n0 + nn], st[:, 0:nn], [st], [Hout], par=True)

    def ln_store(ti, tt, g_t, b_t, scr, final=False):
        st = scr["st"]; xo = scr["xo"]; junk = xo
        A(lambda e: e.activation(junk[:], tt[:], AF.Copy, accum_out=st[:, 0:1]), [tt], [junk, st])
        A(lambda e: e.activation(junk[:], tt[:], AF.Square, accum_out=st[:, 1:2]), [tt], [junk, st])
        V(lambda e: e.tensor_scalar(st[:, 2:3], st[:, 0:1], 1.0 / D, None, ALU.mult), [st], [st])
        V(lambda e: e.tensor_tensor(st[:, 3:4], st[:, 2:3], st[:, 2:3], ALU.mult), [st], [st])
        V(lambda e: e.scalar_tensor_tensor(st[:, 4:5], st[:, 1:2], 1.0 / D, st[:, 3:4], ALU.mult, ALU.subtract), [st], [st])
        A(lambda e: e.activation(st[:, 5:6], st[:, 4:5], AF.Sqrt, bias=1e-5, scale=1.0), [st], [st])
        V(lambda e: e.reciprocal(st[:, 6:7], st[:, 5:6]), [st], [st])
        V(lambda e: e.tensor_scalar(xo[:], tt[:], st[:, 2:3], st[:, 6:7], ALU.subtract, ALU.mult), [tt, st], [xo])
        G(lambda e: e.tensor_tensor(xo[:], xo[:], g_t[:], ALU.mult), [xo, g_t], [xo])
        G(lambda e: e.tensor_tensor(xo[:], xo[:], b_t[:], ALU.add), [xo, b_t], [xo])
        r0 = xrow(ti)
        fw.dma(XR[r0:r0 + 128, :], xo[:], [xo], [XR], par=True)
        if final:
            if ti < NT:
                fw.dma(O["yp"][r0:r0 + 128, :], xo[:], [xo], [O["yp"]], par=True, is_out=True)
            else:
                fw.dma(O["ys"][0:NS, :], xo[0:NS, :], [xo], [O["ys"]], par=True, is_out=True)

    def load_bcast(name, src_ap, n):
        t = fw.sb(name, [128, n])
        fw.dma(t[:], src_ap.partition_broadcast(128), [], [t])
        return t

    def load_wres(name, w_ap):
        t = fw.sb(name, [128, KC, D], BF16)
        reload_wres(t, w_ap)
        return t

    def reload_wres(t, w_ap):
        wv = w_ap.rearrange("(kc p) n -> p kc n", p=128)
        with fw.scope():
            stgw = fw.sb("wres_stg", [128, KC, 512])
            for j in range(0, D, 512):
                fw.dma(stgw[:], wv[:, :, j:j + 512], [], [stgw])
                G(lambda e, j=j: e.tensor_copy(t[:, :, j:j + 512], stgw[:]), [stgw], [t])

    def proj_residual_ln(ti, yT, Wres, g_t, b_t, scr, final=False, rowscale=None):
        xr = scr["xr"]; tt = scr["tt"]
        r0 = xrow(ti)
        fw.dma(xr[:], XR[r0:r0 + 128, :], [XR], [xr])
        for nb in range(NB512):
            ps = pA[nb]
            for kc in range(KC):
                P(lambda e, ps=ps, kc=kc, nb=nb: e.matmul(ps[:], yT[:, kc, :], Wres[:, kc, nb * 512:(nb + 1) * 512], start=(kc == 0), stop=(kc == KC - 1)),
                  [yT, Wres], [ps])
            V(lambda e, ps=ps, nb=nb: e.scalar_tensor_tensor(tt[:, nb * 512:(nb + 1) * 512], xr[:, nb * 512:(nb + 1) * 512], ALPHA, ps[:], ALU.mult, ALU.add),
              [xr, ps], [tt])
        ln_store(ti, tt, g_t, b_t, scr, final)

    def transpose_to(dst, src, nblk, dst_slices=None):
        for g0 in range(0, nblk, 4):
            n = min(4, nblk - g0)
            pt = pT[(g0 // 4) % 2]
            for j in range(n):
                P(lambda e, pt=pt, j=j, b=g0 + j: e.transpose(pt[:, j * 128:(j + 1) * 128], src[:, b * 128:(b + 1) * 128], ident[:]),
                  [src, ident], [pt])
            evac(dst[:, g0:g0 + n, :], pt[:, 0:n * 128].rearrange("p (a b) -> p a b", a=n), [pt], [dst])

    def xattn_stage(l, final):
        ML = c.ML; MC = ML // 128; XH = c.XH; XHD = c.XHD; DC = XHD // 128
        scale = 1.0 / math.sqrt(XHD)
        with fw.scope():
            mkT = [fw.sb("mkT%d" % s, [128, KC, ML], BF16) for s in range(2)]
            mvb = [fw.sb("mvb%d" % s, [128, MC, D], BF16) for s in range(2)]
            with fw.scope():
                memT = fw.sb("memT", [128, KC, ML], BF16)
                mt = fw.sb("mem_t", [128, D])
                for mc in range(MC):
                    fw.dma(mt[:], I["memp"][mc * 128:(mc + 1) * 128, :], [], [mt])
                    for g in range(KC // 4):
                        pt = pT[g % 2]
                        for j in range(4):
                            kc = 4 * g + j
                            P(lambda e, pt=pt, j=j, kc=kc: e.transpose(pt[:, j * 128:(j + 1) * 128], mt[:, kc * 128:(kc + 1) * 128], ident[:]), [mt, ident], [pt])
                        evac(memT[:, 4 * g:4 * g + 4, mc * 128:(mc + 1) * 128], pt[:].rearrange("p (a b) -> p a b", a=4), [pt], [memT])
                wts = [fw.sb("xw%d" % i, [128, KC, 512], BF16) for i in range(2)]
                xwst = fw.sb("xwst", [128, KC, 512])
                mk32 = fw.sb("mk32", [128, 512])
                cnt = 0
                for which, wname, oname in ((0, "wk", "nmk"), (1, "wv", "nmv")):
                    wv_ = I[wname].t.ap()[l].rearrange("(kc p) n -> p kc n", p=128)
                    for n0 in range(0, D, 512):
                        wt = wts[cnt % 2]; cnt += 1
                        fw.dma(xwst[:], wv_[:, :, n0:n0 + 512], [], [xwst])
                        G(lambda e, wt=wt: e.tensor_copy(wt[:], xwst[:]), [xwst], [wt])
                        for mc in range(MC):
                            ps = pA[mc % 4]
                            for kc in range(KC):
                                P(lambda e, ps=ps, kc=kc, wt=wt, mc=mc: e.matmul(ps[:], memT[:, kc, mc * 128:(mc + 1) * 128], wt[:, kc, :], start=(kc == 0), stop=(kc == KC - 1)), [memT, wt], [ps])
                            A(lambda e, ps=ps: e.copy(mk32[:], ps[:]), [ps], [mk32])
                            fw.dma(O[oname][l, mc * 128:(mc + 1) * 128, n0:n0 + 512], mk32[:], [mk32], [O[oname]], par=True, is_out=True)
                            if which == 1:
                                V(lambda e, mc=mc, n0=n0: e.tensor_copy(mvb[0][:, mc, n0:n0 + 512], mk32[:]), [mk32], [mvb[0]])
                            else:
                                pt = pT[mc % 2]
                                for j in range(4):
                                    P(lambda e, pt=pt, j=j: e.transpose(pt[:, j * 128:(j + 1) * 128], mk32[:, j * 128:(j + 1) * 128], ident[:]), [mk32, ident], [pt])
                                kc0 = n0 // 128
                                evac(mkT[0][:, kc0:kc0 + 4, mc * 128:(mc + 1) * 128], pt[:].rearrange("p (a b) -> p a b", a=4), [pt], [mkT[0]])
                for mc in range(MC):
                    fw.dma(mt[:], I["cmk"][l, mc * 128:(mc + 1) * 128, :], [], [mt])
                    for g in range(KC // 4):
                        pt = pT[g % 2]
                        for j in range(4):
                            kc = 4 * g + j
                            P(lambda e, pt=pt, j=j, kc=kc: e.transpose(pt[:, j * 128:(j + 1) * 128], mt[:, kc * 128:(kc + 1) * 128], ident[:]), [mt, ident], [pt])
                        evac(mkT[1][:, 4 * g:4 * g + 4, mc * 128:(mc + 1) * 128], pt[:].rearrange("p (a b) -> p a b", a=4), [pt], [mkT[1]])
                    fw.dma(mt[:], I["cmv"][l, mc * 128:(mc + 1) * 128, :], [], [mt])
                    G(lambda e, mc=mc: e.tensor_copy(mvb[1][:, mc, :], mt[:]), [mt], [mvb[1]])
            Wq = load_wres("Wq", I["wq"].t.ap()[l])
            Wo = Wq
            g_t = load_bcast("lxg", I["lnx_g"][l:l + 1, :], D)
            b_t = load_bcast("lxb", I["lnx_b"][l:l + 1, :], D)
            scr = {"st": fw.sb("xst", [128, 8]), "xo": fw.sb("xxo", [128, D]),
                   "xr": fw.sb("xxr", [128, D]), "tt": fw.sb("xtt", [128, D])}
            xT = fw.sb("x_xT", [128, KC, 128], BF16)
            qT = fw.sb("x_qT", [128, KC, 128], BF16)
            oT = fw.sb("x_oT", [128, KC, 128], BF16)
            pr = fw.sb("x_p", [128, ML])
            pTs = fw.sb("x_pT", [128, MC, 128], BF16)
            sm = fw.sb("x_sm", [128, 8])
            xin = scr["xr"]
            for ti in range(NT + 1):
                s = 0 if ti < NT else 1
                r0 = xrow(ti)
                fw.dma(xin[:], XR[r0:r0 + 128, :], [XR], [xin])
                transpose_to(xT, xin, KC)
                for g in range(KC // 4):
                    ps = pB[g % 2]
                    for j in range(4):
                        cb = 4 * g + j
                        for kc in range(KC):
                            P(lambda e, ps=ps, j=j, cb=cb, kc=kc: e.matmul(ps[:, j * 128:(j + 1) * 128], Wq[:, kc, cb * 128:(cb + 1) * 128], xT[:, kc, :], start=(kc == 0), stop=(kc == KC - 1)),
                              [Wq, xT], [ps])
                    evac(qT[:, 4 * g:4 * g + 4, :], ps[:].rearrange("p (a b) -> p a b", a=4), [ps], [qT])
                for h in range(XH):
                    ps = pA[h % 2]
                    for dc in range(DC):
                        P(lambda e, ps=ps, dc=dc, h=h: e.matmul(ps[:, 0:ML], qT[:, h * DC + dc, :], mkT[s][:, h * DC + dc, :], start=(dc == 0), stop=(dc == DC - 1)),
                          [qT, mkT[s]], [ps])
                    V(lambda e, ps=ps: e.reduce_max(sm[:, 0:1], ps[:, 0:ML], AX.X), [ps], [sm])
                    V(lambda e: e.tensor_scalar(sm[:, 1:2], sm[:, 0:1], -scale, None, ALU.mult), [sm], [sm])
                    A(lambda e, ps=ps: e.activation(pr[:], ps[:, 0:ML], AF.Exp, bias=sm[:, 1:2], scale=scale, accum_out=sm[:, 2:3]), [ps, sm], [pr, sm])
                    V(lambda e: e.reciprocal(sm[:, 3:4], sm[:, 2:3]), [sm], [sm])
                    V(lambda e: e.tensor_scalar(pr[:], pr[:], sm[:, 3:4], None, ALU.mult), [pr, sm], [pr])
                    transpose_to(pTs, pr, MC)
                    po = pB[h % 2]
                    for dc in range(DC):
                        for mc in range(MC):
                            P(lambda e, po=po, dc=dc, mc=mc, h=h: e.matmul(po[:, dc * 128:(dc + 1) * 128], mvb[s][:, mc, (h * DC + dc) * 128:(h * DC + dc + 1) * 128], pTs[:, mc, :], start=(mc == 0), stop=(mc == MC - 1)),
                              [mvb[s], pTs], [po])
                    evac(oT[:, h * DC:(h + 1) * DC, :], po[:, 0:DC * 128].rearrange("p (a b) -> p a b", a=DC), [po], [oT])
                fw.dma(OT[ti, :, :], oT[:].rearrange("p a b -> p (a b)"), [oT], [OT], par=True)
            fw.barrier()
            reload_wres(Wq, I["wo"].t.ap()[l])
            for ti in range(NT + 1):
                fw.dma(oT[:].rearrange("p a b -> p (a b)"), OT[ti, :, :], [OT], [oT])
                proj_residual_ln(ti, oT, Wo, g_t, b_t, scr, final)

    def gate_stage_even(e_):
        l = 2 * e_
        with fw.scope():
            Wout = load_wres("Wout", I["w_out_e"].t.ap()[e_])
            g_t = load_bcast("lmg", I["lnm_g"][l:l + 1, :], D)
            b_t = load_bcast("lmb", I["lnm_b"][l:l + 1, :], D)
            scr = {"st": fw.sb("gst", [128, 8]), "xo": fw.sb("gxo", [128, D]),
                   "xr": fw.sb("gxr", [128, D]), "tt": fw.sb("gtt", [128, D])}
            yt = fw.sb("g_y", [128, D]); zt = fw.sb("g_z", [128, D])
            yT = fw.sb("g_yT", [128, KC, 128], BF16)
            zc0 = 3 * AW + c.BCOLS
            for ti in range(NT + 1):
                r0 = xrow(ti); h0 = herow(ti)
                fw.dma(yt[:], Y[r0:r0 + 128, :], [Y], [yt])
                fw.dma(zt[:], HE[h0:h0 + 128, zc0:zc0 + D], [HE], [zt])
                A(lambda e: e.activation(zt[:], zt[:], AF.Silu), [zt], [zt])
                V(lambda e: e.tensor_tensor(yt[:], yt[:], zt[:], ALU.mult), [yt, zt], [yt])
                transpose_to(yT, yt, KC)
                proj_residual_ln(ti, yT, Wout, g_t, b_t, scr)


    def odd_stage(o):
        l = 2 * o + 1
        PGW = c.PGW; JC = KC // 4
        gemm_f1(I["w_in_o"].t.ap()[o], 2 * D, HO, xrow)
        fw.dma(O["npp"][o, :, :], HO[T - 15:T, 0:D], [HO], [O["npp"]], par=True, is_out=True)
        fw.dma(O["nps"][o, 0:11, :], I["spool"][o, 4:15, :], [], [O["nps"]], par=True, is_out=True)
        fw.dma(O["nps"][o, 11:15, :], HO[T:T + 4, 0:D], [HO], [O["nps"]], par=True, is_out=True)
        with fw.scope():
            Wout = load_wres("WoutO", I["w_out_o"].t.ap()[o])
            g_t = load_bcast("lmgo", I["lnm_g"][l:l + 1, :], D)
            b_t = load_bcast("lmbo", I["lnm_b"][l:l + 1, :], D)
            psc = load_bcast("psc", I["pool_scale"][o:o + 1, :], D)
            PW = fw.sb("PW", [128, 4, JC, PGW], BF16)
            with fw.scope():
                pwst = fw.sb("pwst", [128, JC, PGW])
                for g in range(4):
                    fw.dma(pwst[:], I["pool_w"].t.ap()[o, g].rearrange("(j p) n -> p j n", p=128), [], [pwst])
                    G(lambda e, g=g: e.tensor_copy(PW[:, g, :, :], pwst[:]), [pwst], [PW])
            pms = {}
            for nm in ("pm0", "pmg", "pmp"):
                pms[nm] = fw.sb("s_" + nm, [128, 4, 128])
                fw.dma(pms[nm][:], I[nm][:, :, :], [], [pms[nm]])
            scr = {"st": fw.sb("ost", [128, 8]), "xo": fw.sb("oxo", [128, D]),
                   "xr": fw.sb("oxr", [128, D]), "tt": fw.sb("ott", [128, D])}
            xcs = [fw.sb("o_xc%d" % i, [128, D]) for i in range(2)]
            xsp = scr["tt"]
            zt = fw.sb("o_z", [128, D]); yt = fw.sb("o_y", [128, D])
            pTb = fw.sb("o_pT", [128, KC, 128], BF16)
            yT = fw.sb("o_yT", [128, KC, 128], BF16)
            for ti in range(NT + 1):
                r0 = xrow(ti)
                xc = xcs[ti % 2]
                fw.dma(xc[:], HO[r0:r0 + 128, 0:D], [HO], [xc])
                fw.dma(zt[:], HO[r0:r0 + 128, D:2 * D], [HO], [zt])
                if ti == 0:
                    pm, prev = pms["pm0"], None
                elif ti < NT:
                    pm, prev = pms["pmg"], xcs[(ti - 1) % 2]
                else:
                    pm, prev = pms["pmg"], xsp
                    V(lambda e: e.memset(xsp[:], 0.0), [], [xsp])
                    fw.dma(xsp[113:128, :], I["spool"][o, :, :], [], [xsp])
                for g4 in range(KC // 4):
                    pt = pT[g4 % 2]
                    for j in range(4):
                        cb = 4 * g4 + j
                        g = cb // JC
                        P(lambda e, pt=pt, j=j, cb=cb, g=g, xc=xc, pm=pm, prev=prev: e.matmul(pt[:, j * 128:(j + 1) * 128], xc[:, cb * 128:(cb + 1) * 128], pm[:, g, :], start=True, stop=(prev is None)),
                          [xc, pm], [pt])
                        if prev is not None:
                            P(lambda e, pt=pt, j=j, cb=cb, g=g, prev=prev: e.matmul(pt[:, j * 128:(j + 1) * 128], prev[:, cb * 128:(cb + 1) * 128], pms["pmp"][:, g, :], start=False, stop=True),
                              [prev, pms["pmp"]], [pt])
                    evac(pTb[:, 4 * g4:4 * g4 + 4, :], pt[:].rearrange("p (a b) -> p a b", a=4), [pt], [pTb])
                A(lambda e: e.activation(zt[:], zt[:], AF.Silu), [zt], [zt])
                G(lambda e: e.tensor_tensor(zt[:], zt[:], psc[:], ALU.mult), [zt, psc], [zt])
                for g in range(4):
                    ps = pA[g]
                    for j in range(JC):
                        P(lambda e, ps=ps, g=g, j=j: e.matmul(ps[:, 0:PGW], pTb[:, g * JC + j, :], PW[:, g, j, :], start=(j == 0), stop=(j == JC - 1)), [pTb, PW], [ps])
                    V(lambda e, ps=ps, g=g: e.tensor_tensor(yt[:, g * PGW:(g + 1) * PGW], ps[:, 0:PGW], zt[:, g * PGW:(g + 1) * PGW], ALU.mult), [ps, zt], [yt])
                transpose_to(yT, yt, KC)
                proj_residual_ln(ti, yT, Wout, g_t, b_t, scr)


    def t5_tiles():
        tbs = fw.sb("tbs", [128, AH, 256])
        c31 = fw.sb("c31", [128, AH])
        ncmax = fw.sb("ncmax", [128, AH])
        with fw.scope():
            t5s = fw.sb("t5s", [32, AH]); E = fw.sb("t5Es", [32, 384]); pad = fw.sb("t5pads", [128, 384])
            t5b = fw.sb("t5b", [128, 32 * AH]); rep = fw.sb("t5rep", [32, 128]); zt = fw.sb("t5zt", [128, 384])
            fw.dma(t5s[:], I["t5"][:, :], [], [t5s])
            fw.dma(E[:], I["t5E"][:, :], [], [E])
            fw.dma(pad[:], I["t5pad"][:, :], [], [pad])
            fw.dma(c31[:], I["t5"][31:32, :].partition_broadcast(128), [], [c31])
            fw.dma(t5b[:], I["t5"].t.ap().rearrange("b h -> (b h)").rearrange("(o n) -> o n", o=1).partition_broadcast(128), [], [t5b])
            V(lambda e: e.tensor_reduce(ncmax[:], t5b[:].rearrange("p (b h) -> p h b", h=AH), AX.X, ALU.max), [t5b], [ncmax])
            V(lambda e: e.tensor_scalar(ncmax[:], ncmax[:], -1.0, None, ALU.mult), [ncmax], [ncmax])
            for h in range(AH):
                V(lambda e, h=h: e.tensor_copy(rep[:], t5s[:, h:h + 1].to_broadcast([32, 128])), [t5s], [rep])
                P(lambda e: e.matmul(pB[0][:, 0:384], rep[:], E[:], start=True, stop=True), [rep, E], [pB[0]])
                V(lambda e: e.tensor_tensor(zt[:], pB[0][:, 0:384], pad[:], ALU.add), [pB[0], pad], [zt])
                fw.dma(TZ[h, :, :], zt[:], [zt], [TZ])
                src = bass.AP(TZ.t, h * 128 * 384 + 127, [[383, 128], [1, 256]])
                fw.dma(tbs[:, h, :], src, [TZ], [tbs])
        return tbs, c31, ncmax

    T5C = {}

    def moba_prompt_stage(e_):
        scale = 1.0 / math.sqrt(128.0)
        NBLK = T // 256
        if "t" not in T5C:
            T5C["t"] = t5_tiles()
        tbs, c31, ncmax = T5C["t"]
        if chk('A0'):
            return
        with fw.scope():
            qtm = fw.sb("a_qtm", [128, NT * 128]); ktm = fw.sb("a_ktm", [128, NT * 128]); vtm = fw.sb("a_vtm", [128, NT * 128])
            QT = fw.sb("a_QT", [128, NT, 128], BF16); KT = fw.sb("a_KT", [128, NT, 128], BF16); Vb = fw.sb("a_Vb", [128, NT, 128], BF16)
            km = fw.sb("a_km", [128, 8]); kmb = fw.sb("a_kmb", [128, 8], BF16)
            gs = fw.sb("a_gs", [128, 8]); m8 = fw.sb("a_m8", [128, 8]); mb = fw.sb("a_mb", [128, 8])
            fb = fw.sb("a_fb", [128, 8]); nbm = fw.sb("a_nbm", [128, 8]); sm = fw.sb("a_sm", [128, 8]); negm = fw.sb("a_negm", [128, 4])
            rs = fw.sb("a_rs", [128, NT]); lt = fw.sb("a_lt", [128, 128])
            Pm = fw.sb("a_P", [128, NT * 128]); PT = fw.sb("a_PT", [128, NT, 128], BF16)
            AOh = fw.sb("a_AO", [128, NT, 128])
            for h in range(AH):
                for (dst, c0) in ((qtm, h * 128), (ktm, AW + h * 128), (vtm, 2 * AW + h * 128)):
                    fw.dma(dst[:].rearrange("p (t d) -> p t d", d=128), HE[1:1 + T, c0:c0 + 128].rearrange("(t p) d -> p t d", p=128), [HE], [dst])
                transpose_to(QT, qtm, NT)
                transpose_to(KT, ktm, NT)
                G(lambda e: e.tensor_copy(Vb[:].rearrange("p t d -> p (t d)"), vtm[:]), [vtm], [Vb])
                V(lambda e: e.memset(km[:], 0.0), [], [km])
                V(lambda e: e.tensor_reduce(km[:, 0:NBLK], KT[:].rearrange("p (b u) k -> p b (u k)", u=2), AX.X, ALU.add), [KT], [km])
                V(lambda e: e.tensor_copy(kmb[:], km[:]), [km], [kmb])
                if chk('A1'):
                    return
                for qt in range(NT):
                    i = qt // 2
                    nk = (qt + 1) * 128
                    ncg = (nk + 511) // 512
                    P(lambda e, qt=qt: e.matmul(pB[0][:, 0:8], QT[:, qt, :], kmb[:], start=True, stop=True), [QT, kmb], [pB[0]])
                    if i >= 4:
                        V(lambda e: e.memset(gs[:], -1e30), [], [gs])
                        V(lambda e, i=i: e.tensor_copy(gs[:, 0:i], pB[0][:, 0:i]), [pB[0]], [gs])
                        V(lambda e: e.max(m8[:], gs[:]), [gs], [m8])
                        V(lambda e: e.tensor_scalar(mb[:], gs[:], m8[:, 2:3], None, ALU.is_ge), [gs, m8], [mb])
                        V(lambda e: e.tensor_scalar(mb[:], mb[:], -1.0, 30000.0, ALU.add, ALU.mult), [mb], [mb])
                    else:
                        V(lambda e: e.memset(mb[:], 0.0), [], [mb])
                    if chk('A2') and qt == NT - 1:
                        return
                    for cg in range(ncg):
                        w = min(512, nk - cg * 512)
                        P(lambda e, cg=cg, w=w, qt=qt: e.matmul(pA[cg][:, 0:w], QT[:, qt, :], KT[:, 4 * cg:4 * cg + w // 128, :].rearrange("p t k -> p (t k)"), start=True, stop=True),
                          [QT, KT], [pA[cg]])
                        V(lambda e, cg=cg, w=w: e.reduce_max(sm[:, cg:cg + 1], pA[cg][:, 0:w], AX.X), [pA[cg]], [sm])
                    V(lambda e, ncg=ncg: e.reduce_max(negm[:, 0:1], sm[:, 0:ncg], AX.X), [sm], [negm])
                    V(lambda e, h=h: e.tensor_scalar(negm[:, 1:2], negm[:, 0:1], -scale, ncmax[:, h:h + 1], ALU.mult, ALU.add), [negm, ncmax], [negm])
                    V(lambda e, h=h: e.tensor_scalar(fb[:], mb[:], negm[:, 1:2], c31[:, h:h + 1], ALU.add, ALU.add), [mb, negm, c31], [fb])
                    V(lambda e: e.tensor_scalar(nbm[:], mb[:], negm[:, 1:2], None, ALU.add), [mb, negm], [nbm])
                    for kt in range(qt + 1):
                        j = kt // 2; cg = kt // 4; off = (kt % 4) * 128
                        if kt >= qt - 1:
                            tc0 = (kt - (qt - 1)) * 128
                            V(lambda e, cg=cg, off=off, tc0=tc0, h=h: e.scalar_tensor_tensor(lt[:], pA[cg][:, off:off + 128], scale, tbs[:, h, tc0:tc0 + 128], ALU.mult, ALU.add),
                              [pA[cg], tbs], [lt])
                            bias = nbm[:, j:j + 1] if j < i else negm[:, 1:2]
                            A(lambda e, kt=kt, bias=bias: e.activation(Pm[:, kt * 128:(kt + 1) * 128], lt[:], AF.Exp, bias=bias, scale=1.0, accum_out=rs[:, kt:kt + 1]),
                              [lt, nbm, negm], [Pm, rs])
                        else:
                            A(lambda e, kt=kt, cg=cg, off=off, j=j: e.activation(Pm[:, kt * 128:(kt + 1) * 128], pA[cg][:, off:off + 128], AF.Exp, bias=fb[:, j:j + 1], scale=scale, accum_out=rs[:, kt:kt + 1]),
                              [pA[cg], fb], [Pm, rs])
                    if chk('A3') and qt == NT - 1:
                        return
                    V(lambda e, qt=qt: e.reduce_sum(sm[:, 4:5], rs[:, 0:qt + 1], AX.X), [rs], [sm])
                    V(lambda e: e.reciprocal(sm[:, 5:6], sm[:, 4:5]), [sm], [sm])
                    transpose_to(PT, Pm, qt + 1)
                    for kt in range(qt + 1):
                        P(lambda e, kt=kt, qt=qt: e.matmul(pB[1][:, 0:128], PT[:, kt, :], Vb[:, kt, :], start=(kt == 0), stop=(kt == qt)), [PT, Vb], [pB[1]])
                    V(lambda e, qt=qt: e.tensor_scalar(AOh[:, qt, :], pB[1][:, 0:128], sm[:, 5:6], None, ALU.mult), [pB[1], sm], [AOh])
                fw.dma(Y[0:T, h * 128:(h + 1) * 128].rearrange("(t p) d -> p t d", p=128), AOh[:], [AOh], [Y], par=True)


    def rwkv_stage(e_):
        HP = BH // 2
        xb0 = 3 * AW
        NBK = (BW + 511) // 512
        with fw.scope():
            cur = fw.sb("r_cur", [128, c.BCOLS]); big = fw.sb("r_big", [128, c.BCOLS])
            cA = fw.sb("r_cA", [128, BW]); cB = fw.sb("r_cB", [128, BW])
            mu_t = load_bcast("r_mu", I["mu"][e_:e_ + 1, :], c.BCOLS)
            wl = fw.sb("r_wl", [128, BW]); av = fw.sb("r_a", [128, BW]); kkn = fw.sb("r_kkn", [128, BW])
            kmod = fw.sb("r_kmod", [128, BW]); beta = fw.sb("r_beta", [128, BW]); T4 = fw.sb("r_T4", [128, BW])
            T1 = Buf(big.t[:, 0:BW], "T1v"); T2 = Buf(big.t[:, BW:2 * BW], "T2v"); T3 = Buf(big.t[:, 2 * BW:3 * BW], "T3v")
            for tv in (T1, T2, T3):
                tv.par = big
            RbT = fw.sb("r_RbT", [128, HP, 128]); KtT = fw.sb("r_KtT", [128, HP, 128]); BtT = fw.sb("r_BtT", [128, HP, 128]); AbT = fw.sb("r_AbT", [128, HP, 128])
            MT = fw.sb("r_MT", [128, BH, 384])
            Xs = [fw.sb("r_X%d" % i, [128, BH, 128]) for i in range(2)]
            Zs = [fw.sb("r_Z%d" % i, [128, BH, 128]) for i in range(2)]
            Pm = fw.sb("r_P", [128, BH, 128])
            Yo = fw.sb("r_Yo", [128, BW]); W0s = fw.sb("r_W0", [128, BW]); Us = fw.sb("r_U", [128, BW])
            Hst = [fw.sb("r_H%d" % i, [128, HP, 64]) for i in range(2)]
            gam = fw.sb("r_gam", [128, HP, 2])
            Hm = [fw.sb("r_Hm%d" % i, [128, HP, 64]) for i in range(2)]
            lT = fw.sb("r_lT", [128, 128]); lup = fw.sb("r_lup", [128, BW])
            msk = fw.sb("r_msk", [128, 640]); tri = fw.sb("r_tri", [128, 128]); suf = fw.sb("r_suf", [128, 128]); cind = fw.sb("r_cind", [128, 2])
            st = fw.sb("r_st", [128, 8 * BH]); rowm = fw.sb("r_rowm", [128, 1]); iop = fw.sb("r_iop", [128, 1])
            So = fw.sb("r_So", [64, BH, 64])
            fw.dma(msk[:], I["rw_mask"][:, :], [], [msk]); fw.dma(tri[:], I["tri_incl"][:, :], [], [tri])
            fw.dma(suf[:], I["tri_suf"][:, :], [], [suf]); fw.dma(cind[:], I["chunk_ind"][:, :], [], [cind])
            fw.dma(iop[:], I["iota_p"][:, :], [], [iop])
            V(lambda e: e.tensor_scalar(rowm[:], iop[:], float(NS), None, ALU.is_lt), [iop], [rowm])
            fw.dma(lup[0:64, :], I["w_up"][e_, :, :], [], [lup])
            fw.dma(lup[64:128, :], I["a_up"][e_, :, :], [], [lup], par=True)
            V(lambda e: e.memset(Hst[0][:], 0.0), [], [Hst[0]])
            fw.dma(So[:], I["srw"].t.ap()[e_].rearrange("(h v) k -> v h k", v=64), [], [So])
            for hp in range(HP):
                P(lambda e, hp=hp: e.transpose(pT[0][:, hp * 64:(hp + 1) * 64], So[:, 2 * hp:2 * hp + 2, :].rearrange("v h k -> v (h k)"), ident[0:64, 0:64]), [So, ident], [pT[0]])
            V(lambda e: e.tensor_copy(Hst[1][:], pT[0][:, 0:HP * 64].rearrange("p (a b) -> p a b", b=64)), [pT[0]], [Hst[1]])

            if chk('Ri'):
                return

            def cload(buf, name):
                fw.dma(buf[:], I[name][e_:e_ + 1, :].partition_broadcast(128), [], [buf])

            def hv(t):
                return t[:].rearrange("p (h k) -> p h k", k=64)

            def hb(t2):
                return t2.rearrange("p (h o) -> p h o", o=1).to_broadcast([128, BH, 64])

            for ti in range(NT + 1):
                sample = (ti == NT)
                H = Hst[1] if sample else Hst[0]
                h0 = herow(ti)
                fw.dma(cur[:], HE[h0:h0 + 128, xb0:xb0 + c.BCOLS], [HE], [cur])
                fw.dma(big[:], HE[h0 - 1:h0 + 127, xb0:xb0 + c.BCOLS], [HE], [big])
                G(lambda e: e.tensor_tensor(big[:], big[:], cur[:], ALU.subtract), [big, cur], [big])
                G(lambda e: e.tensor_tensor(big[:], big[:], mu_t[:], ALU.mult), [big, mu_t], [big])
                V(lambda e: e.tensor_tensor(cur[:], cur[:], big[:], ALU.add), [cur, big], [cur])
                rv = cur.t[:, 0:BW]; kv = cur.t[:, BW:2 * BW]; vv = cur.t[:, 2 * BW:3 * BW]
                if chk('R0'):
                    return
                P(lambda e: e.transpose(pT[0][:, 0:128], cur[:, 3 * BW:3 * BW + 128], ident[:]), [cur, ident], [pT[0]])
                A(lambda e: e.activation(lT[0:64, :], pT[0][0:64, 0:128], AF.Tanh), [pT[0]], [lT])
                V(lambda e: e.tensor_copy(lT[64:128, :], pT[0][64:128, 0:128]), [pT[0]], [lT])
                cload(cA, "w0"); cload(cB, "a0")
                for nb in range(NBK):
                    w = min(512, BW - nb * 512)
                    P(lambda e, nb=nb, w=w: e.matmul(pA[nb][:, 0:w], lT[0:64, :], lup[0:64, nb * 512:nb * 512 + w], start=True, stop=True), [lT, lup], [pA[nb]])
                    P(lambda e, nb=nb, w=w: e.matmul(pA[2 + nb][:, 0:w], lT[64:128, :], lup[64:128, nb * 512:nb * 512 + w], start=True, stop=True), [lT, lup], [pA[2 + nb]])
                    V(lambda e, nb=nb, w=w: e.tensor_tensor(wl[:, nb * 512:nb * 512 + w], pA[nb][:, 0:w], cA[:, nb * 512:nb * 512 + w], ALU.add), [pA[nb], cA], [wl])
                    V(lambda e, nb=nb, w=w: e.tensor_tensor(av[:, nb * 512:nb * 512 + w], pA[2 + nb][:, 0:w], cB[:, nb * 512:nb * 512 + w], ALU.add), [pA[2 + nb], cB], [av])
                A(lambda e: e.activation(wl[:], wl[:], AF.Sigmoid), [wl], [wl])
                A(lambda e: e.activation(av[:], av[:], AF.Sigmoid), [av], [av])
                if sample:
                    V(lambda e: e.tensor_scalar(wl[:], wl[:], -0.6065306597126334, rowm[:, 0:1], ALU.mult, ALU.mult), [wl, rowm], [wl])
                else:
                    V(lambda e: e.tensor_scalar(wl[:], wl[:], -0.6065306597126334, None, ALU.mult), [wl], [wl])
                if chk('R1'):
                    return
                cload(cA, "k_k"); cload(cB, "k_a")
                G(lambda e: e.tensor_tensor(kkn[:], kv, cA[:], ALU.mult), [cur, cA], [kkn])
                G(lambda e: e.tensor_tensor(T4[:], kkn[:], kkn[:], ALU.mult), [kkn], [T4])
                V(lambda e: e.tensor_reduce(st[:, 0:BH], hv(T4), AX.X, ALU.add), [T4], [st])
                A(lambda e: e.activation(st[:, 0:BH], st[:, 0:BH], AF.Sqrt), [st], [st])
                V(lambda e: e.tensor_scalar(st[:, 0:BH], st[:, 0:BH], 1e-12, None, ALU.max), [st], [st])
                V(lambda e: e.reciprocal(st[:, BH:2 * BH], st[:, 0:BH]), [st], [st])
                V(lambda e: e.tensor_tensor(hv(kkn), hv(kkn), hb(st[:, BH:2 * BH]), ALU.mult), [kkn, st], [kkn])
                V(lambda e: e.scalar_tensor_tensor(T4[:], av[:], -1.0, cB[:], ALU.add, ALU.mult), [av, cB], [T4])
                V(lambda e: e.scalar_tensor_tensor(kmod[:], T4[:], 1.0, kv, ALU.add, ALU.mult), [T4, cur], [kmod])
                G(lambda e: e.tensor_tensor(beta[:], kkn[:], av[:], ALU.mult), [kkn, av], [beta])
                if sample:
                    V(lambda e: e.tensor_scalar(kmod[:], kmod[:], rowm[:, 0:1], None, ALU.mult), [kmod, rowm], [kmod])
                    V(lambda e: e.tensor_scalar(beta[:], beta[:], rowm[:, 0:1], None, ALU.mult), [beta, rowm], [beta])
                cload(cA, "r_k")
                G(lambda e: e.tensor_tensor(T4[:], rv, kmod[:], ALU.mult), [cur, kmod], [T4])
                G(lambda e: e.tensor_tensor(T4[:], T4[:], cA[:], ALU.mult), [T4, cA], [T4])
                V(lambda e: e.tensor_reduce(st[:, 2 * BH:3 * BH], hv(T4), AX.X, ALU.add), [T4], [st])
                if chk('R2'):
                    return
                for nb in range(NBK):
                    w = min(512, BW - nb * 512)
                    sl = slice(nb * 512, nb * 512 + w)
                    P(lambda e, nb=nb, w=w, sl=sl: e.matmul(pA[nb][:, 0:w], tri[:], wl[:, sl], start=True, stop=True), [tri, wl], [pA[nb]])
                    P(lambda e, nb=nb, w=w, sl=sl: e.matmul(pA[2 + nb][:, 0:w], suf[:], wl[:, sl], start=True, stop=True), [suf, wl], [pA[2 + nb]])
                    A(lambda e, nb=nb, w=w, sl=sl: e.activation(T1[:, sl], pA[nb][:, 0:w], AF.Exp), [pA[nb]], [big])
                    A(lambda e, nb=nb, w=w, sl=sl: e.activation(T2[:, sl], pA[nb][:, 0:w], AF.Exp, scale=-1.0), [pA[nb]], [big])
                    V(lambda e, nb=nb, w=w, sl=sl: e.tensor_tensor(T4[:, sl], pA[nb][:, 0:w], wl[:, sl], ALU.subtract), [pA[nb], wl], [T4])
                A(lambda e: e.activation(T4[:], T4[:], AF.Exp), [T4], [T4])
                V(lambda e: e.scalar_tensor_tensor(T4[:], T4[:], -1.0, kkn[:], ALU.mult, ALU.mult), [T4, kkn], [T4])
                G(lambda e: e.tensor_tensor(T3[:], T2[:], beta[:], ALU.mult), [big, beta], [big])
                V(lambda e: e.tensor_tensor(T2[:], T2[:], kmod[:], ALU.mult), [big, kmod], [big])
                G(lambda e: e.tensor_tensor(T1[:], T1[:], rv, ALU.mult), [big, cur], [big])
                if chk('R3'):
                    return
                for hp in range(HP):
                    P(lambda e, hp=hp: e.matmul(pB[0][:, 2 * hp:2 * hp + 2], wl[:, hp * 128:(hp + 1) * 128], cind[:], start=True, stop=True), [wl, cind], [pB[0]])
                A(lambda e: e.activation(gam[:].rearrange("p a b -> p (a b)"), pB[0][:, 0:2 * HP], AF.Exp), [pB[0]], [gam])
                transpose_to(RbT, T1, HP); transpose_to(KtT, T2, HP); transpose_to(BtT, T3, HP); transpose_to(AbT, T4, HP)
                for nb in range(NBK):
                    w = min(512, BW - nb * 512)
                    sl = slice(nb * 512, nb * 512 + w)
                    A(lambda e, nb=nb, w=w, sl=sl: e.activation(T3[:, sl], pA[2 + nb][:, 0:w], AF.Exp), [pA[2 + nb]], [big])
                V(lambda e: e.tensor_tensor(T1[:], T3[:], kmod[:], ALU.mult), [big, kmod], [big])
                G(lambda e: e.tensor_tensor(T2[:], T3[:], beta[:], ALU.mult), [big, beta], [big])
                Kh, Bh = T1, T2
                if chk('R4'):
                    return
                for h in range(BH):
                    hp, pb = h // 2, (h % 2) * 64
                    ps = pA[h % 2]; pz = pB[h % 2]
                    ops = ((KtT, RbT), (KtT, AbT), (BtT, RbT), (BtT, AbT))
                    for j, (lt_, rt_) in enumerate(ops):
                        P(lambda e, ps=ps, j=j, lt_=lt_, rt_=rt_, hp=hp, pb=pb: e.matmul(ps[:, j * 128:(j + 1) * 128], lt_[pb:pb + 64, hp, :], rt_[pb:pb + 64, hp, :], start=True, stop=True), [lt_, rt_], [ps])
                    P(lambda e, pz=pz, hp=hp, pb=pb: e.matmul(pz[:, 0:128], AbT[pb:pb + 64, hp, :], BtT[pb:pb + 64, hp, :], start=True, stop=True), [AbT, BtT], [pz])
                    V(lambda e, ps=ps, h=h: e.tensor_tensor(MT[:, h, :], ps[:, 0:384], msk[:, 0:384], ALU.mult), [ps, msk], [MT])
                    G_or_V = V
                    V(lambda e, ps=ps, h=h: e.tensor_tensor(Xs[0][:, h, :], ps[:, 384:512], msk[:, 384:512], ALU.mult), [ps, msk], [Xs[0]])
                    V(lambda e, pz=pz, h=h: e.tensor_tensor(Zs[0][:, h, :], pz[:, 0:128], msk[:, 512:640], ALU.mult), [pz, msk], [Zs[0]])
                if chk('R5'):
                    return
                V(lambda e: e.tensor_tensor(Pm[:], Xs[0][:], ident[:].rearrange("p (o k) -> p o k", o=1).to_broadcast([128, BH, 128]), ALU.add), [Xs[0], ident], [Pm])
                if chk('R5a'):
                    return
                cu = 0
                for n in range(1, 1 + int(_os.environ.get('KDBL', '5'))):
                    Xp, Zp = Xs[cu], Zs[cu]; Xn, Zn = Xs[1 - cu], Zs[1 - cu]
                    for g0 in range(0, BH, 4):
                        ng = min(4, BH - g0)
                        pz = pA[(g0 // 4) % 2]
                        for j in range(ng):
                            P(lambda e, pz=pz, j=j, h=g0 + j, Xp=Xp, Zp=Zp: e.matmul(pz[:, j * 128:(j + 1) * 128], Xp[:, h, :], Zp[:, h, :], start=True, stop=True), [Xp, Zp], [pz])
                        evac(Zn[:, g0:g0 + ng, :], pz[:, 0:ng * 128].rearrange("p (a b) -> p a b", b=128), [pz], [Zn])
                        if chk('R5z'):
                            continue
                        if n <= 4:
                            px = pA[2 + (g0 // 4) % 2]
                            for j in range(ng):
                                P(lambda e, px=px, j=j, h=g0 + j, Xp=Xp, Zp=Zp: e.matmul(px[:, j * 128:(j + 1) * 128], Zp[:, h, :], Xp[:, h, :], start=True, stop=True), [Xp, Zp], [px])
                            evac(Xn[:, g0:g0 + ng, :], px[:, 0:ng * 128].rearrange("p (a b) -> p a b", b=128), [px], [Xn])
                        if chk('R5x'):
                            continue
                        pp = pB[(g0 // 4) % 2]
                        for j in range(ng):
                            P(lambda e, pp=pp, j=j, h=g0 + j, Zn=Zn: e.matmul(pp[:, j * 128:(j + 1) * 128], Zn[:, h, :], Pm[:, h, :], start=True, stop=True), [Zn, Pm], [pp])
                        V(lambda e, pp=pp, g0=g0, ng=ng: e.tensor_tensor(Pm[:, g0:g0 + ng, :], Pm[:, g0:g0 + ng, :], pp[:, 0:ng * 128].rearrange("p (a b) -> p a b", b=128), ALU.add), [Pm, pp], [Pm])
                    cu = 1 - cu
                if chk('R6'):
                    return
                NBANK = (BH * 64 + 511) // 512

                def reg(banks, h):
                    return banks[(h * 64) // 512][:, (h * 64) % 512:(h * 64) % 512 + 64]

                for ci in range(2):
                    cs = slice(ci * 64, (ci + 1) * 64)
                    bw_ = [pA[0], pA[1]]; bu_ = [pA[2], pA[3]]; by_ = [pT[0], pT[1]]; bh_ = [pB[0], pB[1]]
                    for par in range(2):
                        V(lambda e, par=par: e.tensor_scalar(Hm[par][:], H[:], cind[:, par:par + 1], None, ALU.mult), [H, cind], [Hm[par]])
                    G(lambda e, ci=ci: e.tensor_scalar(T4[:], cur[:, 2 * BW:3 * BW], cind[:, ci:ci + 1], None, ALU.mult), [cur, cind], [T4])
                    for h in range(BH):
                        hp, par = h // 2, h % 2
                        bk = bw_[(h * 64) // 512]
                        P(lambda e, h=h, hp=hp, par=par: e.matmul(reg(bw_, h), AbT[:, hp, :], Hm[par][:, hp, :], start=True, stop=False), [AbT, Hm[par]], [bk])
                        P(lambda e, h=h: e.matmul(reg(bw_, h), MT[:, h, 128:256], cur[:, 2 * BW + h * 64:2 * BW + (h + 1) * 64], start=False, stop=True), [MT, cur], [bk])
                    for b in range(NBANK):
                        w = min(512, BH * 64 - b * 512)
                        evac(W0s[:, b * 512:b * 512 + w], bw_[b][:, 0:w], [bw_[b]], [W0s])
                    for h in range(BH):
                        bk = bu_[(h * 64) // 512]
                        P(lambda e, h=h: e.matmul(reg(bu_, h), Pm[:, h, :], W0s[:, h * 64:(h + 1) * 64], start=True, stop=True), [Pm, W0s], [bk])
                    for b in range(NBANK):
                        w = min(512, BH * 64 - b * 512)
                        evac(Us[:, b * 512:b * 512 + w], bu_[b][:, 0:w], [bu_[b]], [Us])
                    V(lambda e, ci=ci: e.tensor_scalar(W0s[:], Us[:], cind[:, ci:ci + 1], None, ALU.mult), [Us, cind], [W0s])
                    for h in range(BH):
                        hp, par = h // 2, h % 2
                        bk = by_[(h * 64) // 512]
                        P(lambda e, h=h, hp=hp, par=par: e.matmul(reg(by_, h), RbT[:, hp, :], Hm[par][:, hp, :], start=True, stop=False), [RbT, Hm[par]], [bk])
                        P(lambda e, h=h: e.matmul(reg(by_, h), MT[:, h, 256:384], Us[:, h * 64:(h + 1) * 64], start=False, stop=False), [MT, Us], [bk])
                        P(lambda e, h=h: e.matmul(reg(by_, h), MT[:, h, 0:128], cur[:, 2 * BW + h * 64:2 * BW + (h + 1) * 64], start=False, stop=True), [MT, cur], [bk])
                        bk2 = bh_[(h * 64) // 512]
                        P(lambda e, h=h, hp=hp: e.matmul(reg(bh_, h), Bh[:, hp * 128:(hp + 1) * 128], W0s[:, h * 64:(h + 1) * 64], start=True, stop=False), [big, W0s], [bk2])
                        P(lambda e, h=h, hp=hp: e.matmul(reg(bh_, h), Kh[:, hp * 128:(hp + 1) * 128], T4[:, h * 64:(h + 1) * 64], start=False, stop=True), [big, T4], [bk2])
                    for b in range(NBANK):
                        w = min(512, BH * 64 - b * 512)
                        evac(Yo[cs, b * 512:b * 512 + w], by_[b][cs, 0:w], [by_[b]], [Yo])
                    for b in range(NBANK):
                        w = min(512, BH * 64 - b * 512)
                        nh = w // 128
                        for par in range(2):
                            pb = par * 64
                            hsl = H[pb:pb + 64, b * 4:b * 4 + nh, :]
                            V(lambda e, hsl=hsl, pb=pb, b=b, nh=nh, ci=ci: e.tensor_tensor(hsl, hsl, gam[pb:pb + 64, b * 4:b * 4 + nh, ci:ci + 1].to_broadcast([64, nh, 64]), ALU.mult), [H, gam], [H])
                            V(lambda e, hsl=hsl, pb=pb, b=b, nh=nh, par=par, w=w: e.tensor_tensor(hsl, hsl, bh_[b][pb:pb + 64, 0:w].rearrange("p (q two v) -> p q two v", two=2, v=64)[:, :, par, :], ALU.add), [H, bh_[b]], [H])
                if chk('R7'):
                    return
                cload(cA, "gn_g"); cload(cB, "gn_b")
                V(lambda e: e.tensor_reduce(st[:, 3 * BH:4 * BH], hv(Yo), AX.X, ALU.add), [Yo], [st])
                G(lambda e: e.tensor_tensor(T4[:], Yo[:], Yo[:], ALU.mult), [Yo], [T4])
                V(lambda e: e.tensor_reduce(st[:, 4 * BH:5 * BH], hv(T4), AX.X, ALU.add), [T4], [st])
                V(lambda e: e.tensor_scalar(st[:, 3 * BH:4 * BH], st[:, 3 * BH:4 * BH], 1.0 / 64, None, ALU.mult), [st], [st])
                V(lambda e: e.tensor_tensor(st[:, 5 * BH:6 * BH], st[:, 3 * BH:4 * BH], st[:, 3 * BH:4 * BH], ALU.mult), [st], [st])
                V(lambda e: e.scalar_tensor_tensor(st[:, 4 * BH:5 * BH], st[:, 4 * BH:5 * BH], 1.0 / 64, st[:, 5 * BH:6 * BH], ALU.mult, ALU.subtract), [st], [st])
                A(lambda e: e.activation(st[:, 4 * BH:5 * BH], st[:, 4 * BH:5 * BH], AF.Sqrt, bias=64e-5, scale=1.0), [st], [st])
                V(lambda e: e.reciprocal(st[:, 5 * BH:6 * BH], st[:, 4 * BH:5 * BH]), [st], [st])
                V(lambda e: e.tensor_tensor(hv(Yo), hv(Yo), hb(st[:, 3 * BH:4 * BH]), ALU.subtract), [Yo, st], [Yo])
                V(lambda e: e.tensor_tensor(hv(Yo), hv(Yo), hb(st[:, 5 * BH:6 * BH]), ALU.mult), [Yo, st], [Yo])
                G(lambda e: e.tensor_tensor(Yo[:], Yo[:], cA[:], ALU.mult), [Yo, cA], [Yo])
                G(lambda e: e.tensor_tensor(Yo[:], Yo[:], cB[:], ALU.add), [Yo, cB], [Yo])
                V(lambda e: e.tensor_tensor(hv(T4), cur[:, 2 * BW:3 * BW].rearrange("p (h k) -> p h k", k=64), hb(st[:, 2 * BH:3 * BH]), ALU.mult), [cur, st], [T4])
                G(lambda e: e.tensor_tensor(Yo[:], Yo[:], T4[:], ALU.add), [Yo, T4], [Yo])
                r0 = xrow(ti)
                fw.dma(Y[r0:r0 + 128, AW:AW + BW], Yo[:], [Yo], [Y], par=True)
                if ti == NT - 1 or sample:
                    for hp in range(HP):
                        P(lambda e, hp=hp: e.transpose(pT[0][0:64, (hp % 4) * 128:(hp % 4 + 1) * 128], H[:, hp, :], ident[:]), [H, ident], [pT[0]])
                        if hp % 4 == 3 or hp == HP - 1:
                            g0 = (hp // 4) * 4
                            V(lambda e, g0=g0, hp=hp: e.tensor_copy(So[:, 2 * g0:2 * hp + 2, :].rearrange("v h k -> v (h k)"), pT[0][0:64, 0:(hp - g0 + 1) * 128]), [pT[0]], [So])
                    oname = "nrs" if sample else "nrp"
                    fw.dma(O[oname].t.ap()[e_].rearrange("(h v) k -> v h k", v=64), So[:], [So], [O[oname]], par=True, is_out=True)


    def moba_sample_stage(e_):
        scale = 1.0 / math.sqrt(128.0)
        NPG, NB, PAST = c.NPG, c.NB, c.PAST
        NQ = AH * NS
        if "t" not in T5C:
            T5C["t"] = t5_tiles()
        tbs, c31, ncmax = T5C["t"]
        NAB = (AW + 511) // 512
        with fw.scope():
            KG = dscr("KG%d" % e_, [NPG, 128 * AW]); VG = dscr("VG%d" % e_, [NPG, 128 * AW])
            with fw.scope():
                W = 16 * AW
                CH = 128 * AW // W
                pidx = fw.sb("s_pidx", [NPG, 1], I32); pif = fw.sb("s_pif", [NPG, 1]); jrow = fw.sb("s_jrow", [NPG, CH])
                idxs = fw.sb("s_idxs", [NPG, CH], I32)
                fw.dma(pidx[:], I["ptab"].t.ap().rearrange("o n -> n o"), [], [pidx])
                V(lambda e: e.tensor_copy(pif[:], pidx[:]), [pidx], [pif])
                V(lambda e: e.tensor_scalar(pif[:], pif[:], float(CH), None, ALU.mult), [pif], [pif])
                for j in range(CH):
                    V(lambda e, j=j: e.memset(jrow[:, j:j + 1], float(j + e_ * c.NPHYS * CH)), [], [jrow])
                V(lambda e: e.tensor_scalar(jrow[:], jrow[:], pif[:, 0:1], None, ALU.add), [jrow, pif], [jrow])
                V(lambda e: e.tensor_copy(idxs[:], jrow[:]), [jrow], [idxs])
                gb = [fw.sb("s_gb%d" % i, [NPG, W]) for i in range(2)]
                gi = 0
                for (csrc, gdst) in ((I["ck"], KG), (I["cv"], VG)):
                    src2 = bass.AP(csrc.t, 0, [[W, c.NE * c.NPHYS * CH], [1, W]])
                    for j in range(CH):
                        g = gb[gi % 2]; gi += 1
                        fw.swdma_gather(g, src2, idxs[:, j:j + 1], idxs)
                        fw.dma(gdst[:, j * W:(j + 1) * W], g[:], [g], [gdst], par=True)
            qn = fw.sb("s_qn", [NS, 3 * AW])
            fw.dma(qn[:], HE[T + 2:T + 2 + NS, 0:3 * AW], [HE], [qn])
            qTp = fw.sb("s_qTp", [128, AH, NQ], BF16); knT = fw.sb("s_knT", [128, AH, NS], BF16)
            vnb = fw.sb("s_vnb", [NS, AW], BF16)
            V(lambda e: e.memset(qTp[:], 0.0), [], [qTp])
            for h in range(AH):
                P(lambda e, h=h: e.transpose(pB[0][:, h * NS:(h + 1) * NS], qn[:, h * 128:(h + 1) * 128], ident[0:NS, 0:NS]), [qn, ident], [pB[0]])
                P(lambda e, h=h: e.transpose(pB[1][:, h * NS:(h + 1) * NS], qn[:, AW + h * 128:AW + (h + 1) * 128], ident[0:NS, 0:NS]), [qn, ident], [pB[1]])
            for h in range(AH):
                V(lambda e, h=h: e.tensor_copy(qTp[:, h, h * NS:(h + 1) * NS], pB[0][:, h * NS:(h + 1) * NS]), [pB[0]], [qTp])
            V(lambda e: e.tensor_copy(knT[:].rearrange("p h q -> p (h q)"), pB[1][:, 0:NQ]), [pB[1]], [knT])
            V(lambda e: e.tensor_copy(vnb[:], qn[:, 2 * AW:3 * AW]), [qn], [vnb])
            c31r = fw.sb("s_c31r", [NQ, 1]); ncmr = fw.sb("s_ncmr", [NQ, 1]); TBs = fw.sb("s_TBs", [NQ, 128]); OBt = fw.sb("s_OBt", [NQ, NS])
            for h in range(AH):
                fw.dma(c31r[h * NS:(h + 1) * NS, :], c31[0:NS, h:h + 1], [c31], [c31r], par=True)
                fw.dma(ncmr[h * NS:(h + 1) * NS, :], ncmax[0:NS, h:h + 1], [ncmax], [ncmr], par=True)
                for q in range(NS):
                    r = h * NS + q
                    fw.dma(TBs[r:r + 1, :], TZ[h, 0:1, 127 - q:255 - q], [TZ], [TBs], par=True)
                    fw.dma(OBt[r:r + 1, :], TZ[h, 0:1, 255 - q:255 - q + NS], [TZ], [OBt], par=True)
            kbuf = [fw.sb("s_kb%d" % i, [128, AW]) for i in range(2)]
            KTp = [fw.sb("s_KT%d" % i, [128, AH, 128], BF16) for i in range(2)]
            kmP = fw.sb("s_kmP", [128, AH, NPG])
            SS = fw.sb("s_SS", [NQ, PAST])
            for pg in range(NPG):
                kb = kbuf[pg % 2]; kt = KTp[pg % 2]
                fw.dma(kb[:], KG.t.ap()[pg].rearrange("(t d) -> t d", d=AW), [KG], [kb])
                for h in range(AH):
                    pt = pT[(h // 4) % 2]
                    P(lambda e, pt=pt, h=h, kb=kb: e.transpose(pt[:, (h % 4) * 128:(h % 4 + 1) * 128], kb[:, h * 128:(h + 1) * 128], ident[:]), [kb, ident], [pt])
                for g0 in range(0, AH, 4):
                    ng = min(4, AH - g0)
                    evac(kt[:, g0:g0 + ng, :], pT[(g0 // 4) % 2][:, 0:ng * 128].rearrange("p (a b) -> p a b", b=128), [pT[(g0 // 4) % 2]], [kt])
                V(lambda e, kt=kt, pg=pg: e.tensor_reduce(kmP[:, :, pg], kt[:], AX.X, ALU.add), [kt], [kmP])
                ps = pA[(pg // 4) % 2]
                for h in range(AH):
                    P(lambda e, ps=ps, h=h, kt=kt, pg=pg: e.matmul(ps[0:NQ, (pg % 4) * 128:(pg % 4 + 1) * 128], qTp[:, h, :], kt[:, h, :], start=(h == 0), stop=(h == AH - 1)), [qTp, kt], [ps])
                if pg % 4 == 3:
                    evac(SS[:, (pg - 3) * 128:(pg + 1) * 128], ps[0:NQ, :], [ps], [SS])
            kmT = fw.sb("s_kmT", [128, AH, NB], BF16)
            V(lambda e: e.tensor_tensor(kmT[:], kmP[:].rearrange("p h (n two) -> p h n two", two=2)[:, :, :, 0], kmP[:].rearrange("p h (n two) -> p h n two", two=2)[:, :, :, 1], ALU.add), [kmP], [kmT])
            for h in range(AH):
                P(lambda e, h=h: e.matmul(pB[0][0:NQ, 0:NB], qTp[:, h, :], kmT[:, h, :], start=(h == 0), stop=(h == AH - 1)), [qTp, kmT], [pB[0]])
                P(lambda e, h=h: e.matmul(pB[1][0:NQ, 0:NS], qTp[:, h, :], knT[:, h, :], start=(h == 0), stop=(h == AH - 1)), [qTp, knT], [pB[1]])
            gsb = fw.sb("s_gsb", [NQ, NB]); m8 = fw.sb("s_m8", [NQ, 8]); mbs = fw.sb("s_mb", [NQ, NB]); sm = fw.sb("s_sm", [NQ, 8])
            V(lambda e: e.tensor_copy(gsb[:], pB[0][0:NQ, 0:NB]), [pB[0]], [gsb])
            V(lambda e: e.max(m8[:], gsb[:]), [gsb], [m8])
            V(lambda e: e.tensor_scalar(mbs[:], gsb[:], m8[:, 2:3], None, ALU.is_ge), [gsb, m8], [mbs])
            V(lambda e: e.tensor_scalar(mbs[:], mbs[:], -1.0, 30000.0, ALU.add, ALU.mult), [mbs], [mbs])
            V(lambda e: e.reduce_max(sm[:, 0:1], SS[:], AX.X), [SS], [sm])
            V(lambda e: e.reduce_max(sm[:, 1:2], pB[1][0:NQ, 0:NS], AX.X), [pB[1]], [sm])
            V(lambda e: e.tensor_tensor(sm[:, 0:1], sm[:, 0:1], sm[:, 1:2], ALU.max), [sm], [sm])
            V(lambda e: e.tensor_scalar(sm[:, 2:3], sm[:, 0:1], -scale, ncmr[:, 0:1], ALU.mult, ALU.add), [sm, ncmr], [sm])
            V(lambda e: e.tensor_scalar(mbs[:], mbs[:], sm[:, 2:3], c31r[:, 0:1], ALU.add, ALU.add), [mbs, sm, c31r], [mbs])
            V(lambda e: e.scalar_tensor_tensor(SS[:].rearrange("p (n k) -> p n k", k=256), SS[:].rearrange("p (n k) -> p n k", k=256), scale,
                                                mbs[:].rearrange("p (n o) -> p n o", o=1).to_broadcast([NQ, NB, 256]), ALU.mult, ALU.add), [SS, mbs], [SS])
            V(lambda e: e.tensor_scalar(TBs[:], TBs[:], c31r[:, 0:1], None, ALU.subtract), [TBs, c31r], [TBs])
            V(lambda e: e.tensor_tensor(SS[:, PAST - 128:PAST], SS[:, PAST - 128:PAST], TBs[:], ALU.add), [SS, TBs], [SS])
            A(lambda e: e.activation(SS[:], SS[:], AF.Exp, accum_out=sm[:, 3:4]), [SS], [SS, sm])
            Lo = fw.sb("s_Lo", [NQ, NS])
            V(lambda e: e.scalar_tensor_tensor(Lo[:], pB[1][0:NQ, 0:NS], scale, OBt[:], ALU.mult, ALU.add), [pB[1], OBt], [Lo])
            A(lambda e: e.activation(Lo[:], Lo[:], AF.Exp, bias=sm[:, 2:3], scale=1.0, accum_out=sm[:, 4:5]), [Lo, sm], [Lo, sm])
            V(lambda e: e.tensor_tensor(sm[:, 5:6], sm[:, 3:4], sm[:, 4:5], ALU.add), [sm], [sm])
            V(lambda e: e.reciprocal(sm[:, 6:7], sm[:, 5:6]), [sm], [sm])
            PTs = fw.sb("s_PTs", [128, NPG, NQ], BF16); PoT = fw.sb("s_PoT", [NS, NQ], BF16)
            per = 512 // NQ
            for g0 in range(0, NPG, per):
                ng = min(per, NPG - g0)
                pt = pT[(g0 // per) % 2]
                for j in range(ng):
                    P(lambda e, pt=pt, j=j, pg=g0 + j: e.transpose(pt[:, j * NQ:(j + 1) * NQ], SS[:, pg * 128:(pg + 1) * 128], ident[0:NQ, 0:NQ]), [SS, ident], [pt])
                evac(PTs[:, g0:g0 + ng, :], pt[:, 0:ng * NQ].rearrange("p (a b) -> p a b", b=NQ), [pt], [PTs])
            P(lambda e: e.transpose(pB[0][0:NS, 0:NQ], Lo[:], ident[0:NQ, 0:NQ]), [Lo, ident], [pB[0]])
            V(lambda e: e.tensor_copy(PoT[:], pB[0][0:NS, 0:NQ]), [pB[0]], [PoT])
            vbuf = [fw.sb("s_vb%d" % i, [128, AW]) for i in range(2)]
            Vb = [fw.sb("s_Vb%d" % i, [128, AW], BF16) for i in range(2)]
            for pg in range(NPG):
                vb = vbuf[pg % 2]; vbb = Vb[pg % 2]
                fw.dma(vb[:], VG.t.ap()[pg].rearrange("(t d) -> t d", d=AW), [VG], [vb])
                A(lambda e, vb=vb, vbb=vbb: e.copy(vbb[:], vb[:]), [vb], [vbb])
                for b in range(NAB):
                    w = min(512, AW - b * 512)
                    P(lambda e, b=b, w=w, pg=pg, vbb=vbb: e.matmul(pA[2 + b][0:NQ, 0:w], PTs[:, pg, :], vbb[:, b * 512:b * 512 + w], start=(pg == 0), stop=False), [PTs, vbb], [pA[2 + b]])
            OVs = fw.sb("s_OVs", [NQ, AW])
            for b in range(NAB):
                w = min(512, AW - b * 512)
                P(lambda e, b=b, w=w: e.matmul(pA[2 + b][0:NQ, 0:w], PoT[:], vnb[:, b * 512:b * 512 + w], start=False, stop=True), [PoT, vnb], [pA[2 + b]])
                V(lambda e, b=b, w=w: e.tensor_scalar(OVs[:, b * 512:b * 512 + w], pA[2 + b][0:NQ, 0:w], sm[:, 6:7], None, ALU.mult), [pA[2 + b], sm], [OVs])
            for h in range(AH):
                fw.dma(Y[T:T + NS, h * 128:(h + 1) * 128], OVs[h * NS:(h + 1) * NS, h * 128:(h + 1) * 128], [OVs], [Y], par=True)

    ctx = dict(locals())
    return ctx


def build_full(c, with_cache=False):
    ctx = build(c)
    fw = ctx["fw"]; I = ctx["I"]; O = ctx["O"]; HE = ctx["HE"]; Y = ctx["Y"]
    T, NS, AW, D = c.T, c.NS, c.AW, c.D
    zero = ctx["zero"]
    for j in range(0, D, 512):
        for r0 in range(0, c.R, 128):
            fw.dma(Y[r0:r0 + 128, j:j + 512], zero[:, :], [zero], [Y], par=True)
    fw.barrier()
    for l in range(c.L):
        if l % 2 == 0:
            e = l // 2
            xb0 = 3 * AW
            fw.dma(HE[T + 1:T + 2, xb0:xb0 + c.BCOLS], I["ssh"][e:e + 1, :], [], [HE], par=True)
            ctx["gemm_f1"](I["w_in_e"].t.ap()[e], c.COLS_E, HE, ctx["herow"])
            fw.dma(O["nkp"][e, :, :], HE[1:1 + T, AW:2 * AW], [HE], [O["nkp"]], par=True, is_out=True)
            fw.dma(O["nvp"][e, :, :], HE[1:1 + T, 2 * AW:3 * AW], [HE], [O["nvp"]], par=True, is_out=True)
            fw.dma(O["nks"][e, :, :], HE[T + 2:T + 2 + NS, AW:2 * AW], [HE], [O["nks"]], par=True, is_out=True)
            fw.dma(O["nvs"][e, :, :], HE[T + 2:T + 2 + NS, 2 * AW:3 * AW], [HE], [O["nvs"]], par=True, is_out=True)
            fw.dma(O["nshp"][e:e + 1, :], HE[T:T + 1, xb0:xb0 + c.BCOLS], [HE], [O["nshp"]], par=True, is_out=True)
            fw.dma(O["nshs"][e:e + 1, :], HE[T + 1 + NS:T + 2 + NS, xb0:xb0 + c.BCOLS], [HE], [O["nshs"]], par=True, is_out=True)
            for st in ("moba_prompt_stage", "moba_sample_stage", "rwkv_stage"):
                if st in ctx:
                    ctx[st](e)
            ctx["gate_stage_even"](e)
        else:
            ctx["odd_stage"](l // 2)
        ctx["xattn_stage"](l, l == c.L - 1)
    fw.finish()
    return fw, ctx


_IN_MAP = dict(
    w_in_e="w_in_even", w_out_e="w_out_even", mu="rwkv_mu", w0="rwkv_w0", w_up="rwkv_w_up", a0="rwkv_a0", a_up="rwkv_a_up",
    k_k="rwkv_k_k", k_a="rwkv_k_a", gn_g="rwkv_gn_g", gn_b="rwkv_gn_b", t5="t5_bias", w_in_o="w_in_odd", pool_w="pool_w",
    pool_scale="pool_scale", w_out_o="w_out_odd", wq="xattn_w_q", wk="xattn_w_k", wv="xattn_w_v", wo="xattn_w_o",
    lnm_g="ln_mix_g", lnm_b="ln_mix_b", lnx_g="ln_x_g", lnx_b="ln_x_b")


def kernel(**inp):
    from concourse.bass_utils import run_bass_kernel_spmd
    c = Cfg()
    fw, ctx = build_full(c)
    used = set(ctx["I"].keys())
    hc = host_consts(c)
    A = lambda k: np.ascontiguousarray(np.asarray(inp[k]))
    shared = {}
    for k, src in _IN_MAP.items():
        shared[k] = A(src)
    shared["r_k"] = A("rwkv_r_k").reshape(c.NE, c.BW)
    for k, v in hc.items():
        shared[k] = v
    in_maps = []
    for core in range(8):
        b = core % 4
        m = dict(shared)
        m["xp"] = A("x_prompt")[b]
        m["xs"] = A("x_sample")[core]
        m["ptab"] = A("page_table")[core:core + 1].astype(np.int32)
        m["srw"] = A("state_rwkv")[:, core].reshape(c.NE, c.BH * 64, 64)
        m["ssh"] = A("state_shift")[:, core]
        m["spool"] = A("state_pool")[:, core]
        m["cmk"] = A("cache_mem_k")[:, core].reshape(c.L, c.ML, c.D)
        m["cmv"] = A("cache_mem_v")[:, core].reshape(c.L, c.ML, c.D)
        m["memp"] = A("mem_prompt")[b]
        if "ck" in used:
            m["ck"] = A("cache_moba_k").reshape(c.NE, c.NPHYS * 128, c.AW)
            m["cv"] = A("cache_moba_v").reshape(c.NE, c.NPHYS * 128, c.AW)
        in_maps.append({k: np.ascontiguousarray(v) for k, v in m.items() if k in used})
    res = run_bass_kernel_spmd(fw.nc, in_maps, core_ids=list(range(8)))
    R = res.results
    st = lambda name, cores: np.stack([R[i][name] for i in cores], 0)
    P4 = range(4); S8 = range(8)
    NE, NO, L, T, NS, D = c.NE, c.NO, c.L, c.T, c.NS, c.D
    y_prompt = st("yp", P4)
    y_sample = st("ys", S8)
    nkp = st("nkp", P4).transpose(1, 0, 2, 3).reshape(NE, 4, T, c.AH, 128)
    nvp = st("nvp", P4).transpose(1, 0, 2, 3).reshape(NE, 4, T, c.AH, 128)
    nrp = st("nrp", P4).transpose(1, 0, 2, 3).reshape(NE, 4, c.BH, 64, 64)
    nshp = st("nshp", P4).transpose(1, 0, 2)
    npp = st("npp", P4).transpose(1, 0, 2, 3)
    nmk = st("nmk", P4).transpose(1, 0, 2, 3).reshape(L, 4, c.ML, c.XH, c.XHD)
    nmv = st("nmv", P4).transpose(1, 0, 2, 3).reshape(L, 4, c.ML, c.XH, c.XHD)
    nks = st("nks", S8).transpose(1, 0, 2, 3).reshape(NE, 8, NS, c.AH, 128)
    nvs = st("nvs", S8).transpose(1, 0, 2, 3).reshape(NE, 8, NS, c.AH, 128)
    nrs = st("nrs", S8).transpose(1, 0, 2, 3).reshape(NE, 8, c.BH, 64, 64)
    nshs = st("nshs", S8).transpose(1, 0, 2)
    nps = st("nps", S8).transpose(1, 0, 2, 3)
    outs = (y_prompt, y_sample, nkp, nvp, nrp, nshp, npp, nmk, nmv, nks, nvs, nrs, nshs, nps)
    return tuple(np.ascontiguousarray(o.astype(np.float32)) for o in outs)
```

```python
import numpy as np
from contextlib import ExitStack
import concourse.bass as bass
import concourse.mybir as mybir

F32 = mybir.dt.float32
BF16 = mybir.dt.bfloat16
I32 = mybir.dt.int32
AF = mybir.ActivationFunctionType
ALU = mybir.AluOpType
AX = mybir.AxisListType


class Buf:
    __slots__ = ("t", "name", "w", "r", "dram", "par", "sem")

    def __init__(self, t, name, dram=False):
        self.t = t
        self.name = name
        self.w = []
        self.r = {}
        self.dram = dram
        self.par = None
        self.sem = None

    def __getitem__(self, k):
        return self.t[k]


class FW:
    EPOCH = 12000

    def __init__(self, nc):
        self.nc = nc
        self.es = ExitStack()
        self.perm = ExitStack()
        self.eng = {"pe": nc.tensor, "act": nc.scalar, "dve": nc.vector, "pool": nc.gpsimd, "sp": nc.sync}
        self.sem = {}
        self.cnt = {}
        self.nsem = 0
        for e in self.eng:
            self._new_epoch(e)
        self.waited = {e: {} for e in self.eng}
        self.dpool = []
        self.dcnt = []
        self.dlast = []
        for i in range(8):
            s = self.perm.enter_context(nc.semaphore("dq%d" % i))
            self.dpool.append(s)
            self.dcnt.append(0)
            self.dlast.append(None)
        self.dnext = 0
        self.ninstr = 0
        self.out_tickets = []

    def _new_epoch(self, e):
        s = self.perm.enter_context(self.nc.semaphore("e_%s_%d" % (e, self.nsem)))
        self.nsem += 1
        self.sem[e] = s
        self.cnt[e] = 0

    def sb(self, name, shape, dt=F32):
        self.nsem += 1
        name = "%s_u%d" % (name, self.nsem)
        t = self.es.enter_context(self.nc.sbuf_tensor(name, list(shape), dt))
        return Buf(t, name)

    def ps(self, name, shape, dt=F32):
        t = self.es.enter_context(self.nc.psum_tensor(name, list(shape), dt))
        return Buf(t, name)

    def dram(self, name, shape, dt=F32, kind="Internal"):
        t = self.nc.dram_tensor(name, list(shape), dt, kind=kind)
        return Buf(t, name, dram=True)

    def _wait(self, e, tk):
        if tk is None:
            return
        sem, val, src = tk
        if src == e and e in ("pe", "sp"):
            return
        k = id(sem)
        if self.waited[e].get(k, -1) >= val:
            return
        self.eng[e].wait_ge(sem, val)
        self.waited[e][k] = val

    def _deps(self, e, reads, writes, par=False):
        reads = [b.par or b for b in reads]
        writes = [b.par or b for b in writes]
        for b in reads:
            for tk in b.w:
                self._wait(e, tk)
        for b in writes:
            if not par:
                for tk in b.w:
                    self._wait(e, tk)
            for tk in b.r.values():
                self._wait(e, tk)

    def _record(self, tk, reads, writes, par=False):
        reads = [b.par or b for b in reads]
        writes = [b.par or b for b in writes]
        for b in writes:
            if par:
                b.w.append(tk)
            else:
                b.w = [tk]
            b.r = {}
        for b in reads:
            if b in writes:
                continue
            b.r[id(tk[0])] = tk

    def op(self, e, fn, reads=(), writes=(), accum=False):
        self._deps(e, reads, writes)
        ins = fn(self.eng[e])
        if self.cnt[e] >= self.EPOCH:
            self._new_epoch(e)
        self.cnt[e] += 1
        ins.then_inc(self.sem[e], 1)
        tk = (self.sem[e], self.cnt[e], e)
        self._record(tk, reads, writes)
        self.ninstr += 1
        return tk

    def dma(self, out_ap, in_ap, reads=(), writes=(), q="sp", is_out=False, indirect=None, par=False):
        i = self.dnext
        self.dnext = (self.dnext + 1) % len(self.dpool)
        self._wait(q, self.dlast[i])
        self._deps(q, reads, writes, par)
        if indirect is not None:
            ins = self.eng[q].indirect_dma_start(out=out_ap, out_offset=None, in_=in_ap, in_offset=indirect)
        else:
            ins = self.eng[q].dma_start(out=out_ap, in_=in_ap)
        self.dcnt[i] += 16
        ins.then_inc(self.dpool[i], 16)
        tk = (self.dpool[i], self.dcnt[i], "dma")
        self.dlast[i] = tk
        self._record(tk, reads, writes, par)
        if is_out:
            self.out_tickets.append(tk)
        self.ninstr += 1
        return tk

    def swdma_gather(self, buf, in_ap, idx_ap, idxbuf, out_ap=None, element_offset=0):
        self.nsem += 1
        sem = self.perm.enter_context(self.nc.semaphore("sw%d" % self.nsem))
        self._deps("pool", [idxbuf], [buf])
        ins = self.eng["pool"].indirect_dma_start(out=(out_ap if out_ap is not None else buf[:]), out_offset=None, in_=in_ap,
                                                  in_offset=bass.IndirectOffsetOnAxis(ap=idx_ap, axis=0), element_offset=element_offset)
        ins.then_inc(sem, 16)
        tk = (sem, 16, "dma")
        self._record(tk, [idxbuf], [buf])
        self.ninstr += 1
        return tk

    def finish(self):
        for tk in self.dlast:
            self._wait("sp", tk)
        for e in ("pe", "act", "dve", "pool"):
            if self.cnt[e] > 0:
                self._wait("sp", (self.sem[e], self.cnt[e], e))

    def close(self):
        self.es.close()
        self.perm.close()


def _fw_scope(self):
    from contextlib import contextmanager

    @contextmanager
    def cm():
        old = self.es
        self.es = ExitStack()
        try:
            yield
        finally:
            self.barrier()
            self.es.close()
            self.es = old
    return cm()


def _fw_barrier(self):
    tks = []
    for e in ("pe", "act", "dve", "pool"):
        if self.cnt[e] > 0:
            tks.append((self.sem[e], self.cnt[e], e))
    for tk in self.dlast:
        if tk is not None:
            tks.append(tk)
    for e in self.eng:
        for tk in tks:
            if tk[2] == e and e != "dma":
                if e in ("pe", "sp"):
                    continue
            self._wait(e, tk)


FW.scope = _fw_scope
FW.barrier = _fw_barrier


import math


class Cfg:
    def __init__(s, D=2048, T=2048, NS=4, PAST=16384, ML=256, NPHYS=1280, DEPTH=4):
        s.D = D; s.T = T; s.NS = NS; s.PAST = PAST; s.ML = ML; s.NPHYS = NPHYS; s.L = DEPTH
        s.AW = D // 2; s.AH = s.AW // 128; s.BW = D // 2; s.BH = s.BW // 64
        s.BCOLS = 3 * s.BW + 128; s.COLS_E = 3 * s.AW + s.BCOLS + D; s.COLS_O = 2 * D
        s.XH = 4; s.XHD = D // 4; s.PGW = D // 4
        s.NT = T // 128; s.KC = D // 128; s.NPG = PAST // 128; s.NB = PAST // 256
        s.NE = (DEPTH + 1) // 2; s.NO = DEPTH // 2
        s.R = T + 128
        s.HR = T + 2 + 128
        s.ALPHA = (2 * DEPTH) ** 0.25


def t5_bucket_np(rel):
    rel = np.maximum(rel, 0)
    relf = np.maximum(rel, 16).astype(np.float32)
    large = 16 + (np.log(relf / np.float32(16)) / np.float32(math.log(128 / 16)) * np.float32(16)).astype(np.int32)
    large = np.minimum(large, 31)
    return np.where(rel < 16, rel, large)


def host_consts(c):
    k = {}
    k["ident"] = np.eye(128, dtype=np.float32)
    E = np.zeros((32, 384), np.float32)
    pad = np.zeros((128, 384), np.float32)
    for m in range(256):
        E[int(t5_bucket_np(np.array(255 - m))), m] = 1.0
    pad[:, 256:] = -1e30
    k["t5E"] = E
    k["t5pad"] = pad
    k["iota_p"] = np.arange(128, dtype=np.float32).reshape(128, 1)
    s_ = np.arange(128)[:, None]; t_ = np.arange(128)[None, :]
    same = (s_ // 64) == (t_ // 64)
    up_incl = (same & (s_ <= t_)).astype(np.float32)
    up_strict = (same & (s_ < t_)).astype(np.float32)
    lo_strict = (same & (s_ > t_)).astype(np.float32)
    k["rw_mask"] = np.concatenate([up_incl, up_strict, up_incl, up_strict, lo_strict], axis=1)
    k["tri_incl"] = up_incl
    k["tri_suf"] = lo_strict
    chunk_ind = np.zeros((128, 2), np.float32); chunk_ind[:64, 0] = 1; chunk_ind[64:, 1] = 1
    k["chunk_ind"] = chunk_ind
    wins = (2, 4, 8, 16)
    pm0 = np.zeros((4, 128, 128), np.float32); pmg = np.zeros((4, 128, 128), np.float32); pmp = np.zeros((4, 128, 128), np.float32)
    for g, w in enumerate(wins):
        for t in range(128):
            for s in range(max(0, t - w + 1), t + 1):
                pmg[g, s, t] = 1.0 / w
                pm0[g, s, t] = 1.0 / min(w, t + 1)
            pmg[g, t, t] -= 1.0
            pm0[g, t, t] -= 1.0
            for s in range(t - w + 1, 0):
                pmp[g, 128 + s, t] = 1.0 / w
    k["pm0"] = pm0.transpose(1, 0, 2).copy()
    k["pmg"] = pmg.transpose(1, 0, 2).copy()
    k["pmp"] = pmp.transpose(1, 0, 2).copy()
    return k


def build(c, debug_outs=()):
    nc = bass.Bass("TRN2", target_bir_lowering=False)
    fw = FW(nc)
    D, T, NS, KC, NT, AW, BW, AH, BH = c.D, c.T, c.NS, c.KC, c.NT, c.AW, c.BW, c.AH, c.BH
    NB512 = D // 512
    ALPHA = c.ALPHA

    def din(name, shape, dt=F32):
        return fw.dram(name, shape, dt, kind="ExternalInput")

    def dout(name, shape):
        return fw.dram(name, shape, F32, kind="ExternalOutput")

    def dscr(name, shape, dt=F32):
        return fw.dram(name, shape, dt, kind=("ExternalOutput" if name in debug_outs else "Internal"))

    I = {}
    for name, shape in [
        ("xp", [T, D]), ("xs", [NS, D]), ("ck", [c.NE, c.NPHYS * 128, AW]), ("cv", [c.NE, c.NPHYS * 128, AW]),
        ("srw", [c.NE, BH * 64, 64]), ("ssh", [c.NE, c.BCOLS]), ("spool", [c.NO, 15, D]),
        ("cmk", [c.L, c.ML, D]), ("cmv", [c.L, c.ML, D]), ("memp", [c.ML, D]),
        ("w_in_e", [c.NE, D, c.COLS_E]), ("w_out_e", [c.NE, D, D]), ("mu", [c.NE, c.BCOLS]),
        ("w0", [c.NE, BW]), ("w_up", [c.NE, 64, BW]), ("a0", [c.NE, BW]), ("a_up", [c.NE, 64, BW]),
        ("k_k", [c.NE, BW]), ("k_a", [c.NE, BW]), ("r_k", [c.NE, BW]), ("gn_g", [c.NE, BW]), ("gn_b", [c.NE, BW]),
        ("t5", [32, AH]), ("w_in_o", [c.NO, D, 2 * D]), ("pool_w", [c.NO, 4, c.PGW, c.PGW]), ("pool_scale", [c.NO, D]),
        ("w_out_o", [c.NO, D, D]), ("wq", [c.L, D, D]), ("wk", [c.L, D, D]), ("wv", [c.L, D, D]), ("wo", [c.L, D, D]),
        ("lnm_g", [c.L, D]), ("lnm_b", [c.L, D]), ("lnx_g", [c.L, D]), ("lnx_b", [c.L, D]),
        ("ident", [128, 128]), ("t5E", [32, 384]), ("t5pad", [128, 384]), ("iota_p", [128, 1]),
        ("rw_mask", [128, 640]), ("tri_incl", [128, 128]), ("tri_suf", [128, 128]), ("chunk_ind", [128, 2]),
        ("pm0", [128, 4, 128]), ("pmg", [128, 4, 128]), ("pmp", [128, 4, 128]),
    ]:
        I[name] = din(name, shape)
    I["ptab"] = din("ptab", [1, c.NPG], I32)
    O = {}
    for name, shape in [
        ("yp", [T, D]), ("ys", [NS, D]), ("nkp", [c.NE, T, AW]), ("nvp", [c.NE, T, AW]), ("nrp", [c.NE, BH * 64, 64]),
        ("nshp", [c.NE, c.BCOLS]), ("npp", [c.NO, 15, D]), ("nmk", [c.L, c.ML, D]), ("nmv", [c.L, c.ML, D]),
        ("nks", [c.NE, NS, AW]), ("nvs", [c.NE, NS, AW]), ("nrs", [c.NE, BH * 64, 64]), ("nshs", [c.NE, c.BCOLS]),
        ("nps", [c.NO, 15, D]),
    ]:
        O[name] = dout(name, shape)
    XR = dscr("XR", [c.R, D])
    HE = dscr("HE", [c.HR, c.COLS_E])
    HO = dscr("HO", [c.R, 2 * D])
    Y = dscr("Y", [c.R, D])
    TZ = dscr("TZ", [AH, 128, 384])
    OT = dscr("OT", [NT + 1, 128, KC * 128], BF16)

    import os as _os
    _stop = _os.environ.get('KSTOP', '')

    def chk(tag):
        return tag in _stop.split(',')

    def V(fn, r=(), w=()):
        return fw.op("dve", fn, r, w)

    def A(fn, r=(), w=()):
        return fw.op("act", fn, r, w)

    def G(fn, r=(), w=()):
        return fw.op("pool", fn, r, w)

    def P(fn, r=(), w=()):
        return fw.op("pe", fn, r, w)

    ident = fw.sb("ident_s", [128, 128])
    fw.dma(ident[:], I["ident"][:, :], [I["ident"]], [ident])
    zero = fw.sb("zero_s", [128, 512])
    V(lambda e: e.memset(zero[:], 0.0), [], [zero])
    pA = [fw.ps("pA%d" % i, [128, 512]) for i in range(4)]
    pT = [fw.ps("pT%d" % i, [128, 512]) for i in range(2)]
    pB = [fw.ps("pB%d" % i, [128, 512]) for i in range(2)]
    rr = {"ev": 0}

    def evac(out_ap, in_ap, r, w):
        rr["ev"] ^= 1
        if rr["ev"]:
            return A(lambda e: e.copy(out_ap, in_ap), r, w)
        return V(lambda e: e.tensor_copy(out_ap, in_ap), r, w)

    def herow(ti):
        return 1 + 128 * ti if ti < NT else T + 2

    def xrow(ti):
        return 128 * ti

    fw.dma(XR[0:T, :], I["xp"][:, :], [I["xp"]], [XR])
    fw.dma(XR[T:T + NS, :], I["xs"][:, :], [I["xs"]], [XR], par=True)
    for j in range(0, D, 512):
        fw.dma(XR[T + NS:c.R, j:j + 512], zero[0:128 - NS, :], [zero], [XR], par=True)
    for j in range(0, c.COLS_E, 512):
        n = min(512, c.COLS_E - j)
        fw.dma(HE[0:1, j:j + n], zero[0:1, 0:n], [zero], [HE], par=True)
    fw.barrier()

    def gemm_f1(w_ap, N, Hout, rowfn):
        with fw.scope():
            XT = fw.sb("XT", [128, KC, (NT + 1) * 128], BF16)
            xr_t = [fw.sb("f1xr%d" % i, [128, D]) for i in range(2)]
            for ti in range(NT + 1):
                xt = xr_t[ti % 2]
                fw.dma(xt[:], XR[xrow(ti):xrow(ti) + 128, :], [XR], [xt])
                for g in range(KC // 4):
                    pt = pT[g % 2]
                    for j in range(4):
                        kc = 4 * g + j
                        P(lambda e, pt=pt, j=j, kc=kc, xt=xt: e.transpose(pt[:, j * 128:(j + 1) * 128], xt[:, kc * 128:(kc + 1) * 128], ident[:]),
                          [xt, ident], [pt])
                    evac(XT[:, 4 * g:4 * g + 4, ti * 128:(ti + 1) * 128], pt[:].rearrange("p (a b) -> p a b", a=4), [pt], [XT])
            wts = [fw.sb("f1w%d" % i, [128, KC, 512], BF16) for i in range(2)]
            wst = fw.sb("f1wst", [128, KC, 512])
            stg = [fw.sb("f1s%d" % i, [128, 512]) for i in range(3)]
            wv = w_ap.rearrange("(kc p) n -> p kc n", p=128)
            cnt = 0
            for gi, n0 in enumerate(range(0, N, 512)):
                nn = min(512, N - n0)
                wt = wts[gi % 2]
                fw.dma(wst[:, :, 0:nn], wv[:, :, n0:n0 + nn], [], [wst])
                G(lambda e, wt=wt, nn=nn: e.tensor_copy(wt[:, :, 0:nn], wst[:, :, 0:nn]), [wst], [wt])
                for ti in range(NT + 1):
                    ps = pA[cnt % 4]
                    st = stg[cnt % 3]
                    cnt += 1
                    for kc in range(KC):
                        P(lambda e, ps=ps, kc=kc, wt=wt, ti=ti: e.matmul(ps[:, 0:nn], XT[:, kc, ti * 128:(ti + 1) * 128], wt[:, kc, 0:nn], start=(kc == 0), stop=(kc == KC - 1)),
                          [XT, wt], [ps])
                    evac(st[:, 0:nn], ps[:, 0:nn], [ps], [st])
                    r0 = rowfn(ti)
                    fw.dma(Hout[r0:r0 + 128, n0:n0 + nn], st[:, 0:nn], [st], [Hout], par=True)

    def ln_store(ti, tt, g_t, b_t, scr, final=False):
        st = scr["st"]; xo = scr["xo"]; junk = xo
        A(lambda e: e.activation(junk[:], tt[:], AF.Copy, accum_out=st[:, 0:1]), [tt], [junk, st])
        A(lambda e: e.activation(junk[:], tt[:], AF.Square, accum_out=st[:, 1:2]), [tt], [junk, st])
        V(lambda e: e.tensor_scalar(st[:, 2:3], st[:, 0:1], 1.0 / D, None, ALU.mult), [st], [st])
        V(lambda e: e.tensor_tensor(st[:, 3:4], st[:, 2:3], st[:, 2:3], ALU.mult), [st], [st])
        V(lambda e: e.scalar_tensor_tensor(st[:, 4:5], st[:, 1:2], 1.0 / D, st[:, 3:4], ALU.mult, ALU.subtract), [st], [st])
        A(lambda e: e.activation(st[:, 5:6], st[:, 4:5], AF.Sqrt, bias=1e-5, scale=1.0), [st], [st])
        V(lambda e: e.reciprocal(st[:, 6:7], st[:, 5:6]), [st], [st])
        V(lambda e: e.tensor_scalar(xo[:], tt[:], st[:, 2:3], st[:, 6:7], ALU.subtract, ALU.mult), [tt, st], [xo])
        G(lambda e: e.tensor_tensor(xo[:], xo[:], g_t[:], ALU.mult), [xo, g_t], [xo])
        G(lambda e: e.tensor_tensor(xo[:], xo[:], b_t[:], ALU.add), [xo, b_t], [xo])
        r0 = xrow(ti)
        fw.dma(XR[r0:r0 + 128, :], xo[:], [xo], [XR], par=True)
        if final:
            if ti < NT:
                fw.dma(O["yp"][r0:r0 + 128, :], xo[:], [xo], [O["yp"]], par=True, is_out=True)
            else:
                fw.dma(O["ys"][0:NS, :], xo[0:NS, :], [xo], [O["ys"]], par=True, is_out=True)

    def load_bcast(name, src_ap, n):
        t = fw.sb(name, [128, n])
        fw.dma(t[:], src_ap.partition_broadcast(128), [], [t])
        return t

    def load_wres(name, w_ap):
        t = fw.sb(name, [128, KC, D], BF16)
        reload_wres(t, w_ap)
        return t

    def reload_wres(t, w_ap):
        wv = w_ap.rearrange("(kc p) n -> p kc n", p=128)
        with fw.scope():
            stgw = fw.sb("wres_stg", [128, KC, 512])
            for j in range(0, D, 512):
                fw.dma(stgw[:], wv[:, :, j:j + 512], [], [stgw])
                G(lambda e, j=j: e.tensor_copy(t[:, :, j:j + 512], stgw[:]), [stgw], [t])

    def proj_residual_ln(ti, yT, Wres, g_t, b_t, scr, final=False, rowscale=None):
        xr = scr["xr"]; tt = scr["tt"]
        r0 = xrow(ti)
        fw.dma(xr[:], XR[r0:r0 + 128, :], [XR], [xr])
        for nb in range(NB512):
            ps = pA[nb]
            for kc in range(KC):
                P(lambda e, ps=ps, kc=kc, nb=nb: e.matmul(ps[:], yT[:, kc, :], Wres[:, kc, nb * 512:(nb + 1) * 512], start=(kc == 0), stop=(kc == KC - 1)),
                  [yT, Wres], [ps])
            V(lambda e, ps=ps, nb=nb: e.scalar_tensor_tensor(tt[:, nb * 512:(nb + 1) * 512], xr[:, nb * 512:(nb + 1) * 512], ALPHA, ps[:], ALU.mult, ALU.add),
              [xr, ps], [tt])
        ln_store(ti, tt, g_t, b_t, scr, final)

    def transpose_to(dst, src, nblk, dst_slices=None):
        for g0 in range(0, nblk, 4):
            n = min(4, nblk - g0)
            pt = pT[(g0 // 4) % 2]
            for j in range(n):
                P(lambda e, pt=pt, j=j, b=g0 + j: e.transpose(pt[:, j * 128:(j + 1) * 128], src[:, b * 128:(b + 1) * 128], ident[:]),
                  [src, ident], [pt])
            evac(dst[:, g0:g0 + n, :], pt[:, 0:n * 128].rearrange("p (a b) -> p a b", a=n), [pt], [dst])

    def xattn_stage(l, final):
        ML = c.ML; MC = ML // 128; XH = c.XH; XHD = c.XHD; DC = XHD // 128
        scale = 1.0 / math.sqrt(XHD)
        with fw.scope():
            mkT = [fw.sb("mkT%d" % s, [128, KC, ML], BF16) for s in range(2)]
            mvb = [fw.sb("mvb%d" % s, [128, MC, D], BF16) for s in range(2)]
            with fw.scope():
                memT = fw.sb("memT", [128, KC, ML], BF16)
                mt = fw.sb("mem_t", [128, D])
                for mc in range(MC):
                    fw.dma(mt[:], I["memp"][mc * 128:(mc + 1) * 128, :], [], [mt])
                    for g in range(KC // 4):
                        pt = pT[g % 2]
                        for j in range(4):
                            kc = 4 * g + j
                            P(lambda e, pt=pt, j=j, kc=kc: e.transpose(pt[:, j * 128:(j + 1) * 128], mt[:, kc * 128:(kc + 1) * 128], ident[:]), [mt, ident], [pt])
                        evac(memT[:, 4 * g:4 * g + 4, mc * 128:(mc + 1) * 128], pt[:].rearrange("p (a b) -> p a b", a=4), [pt], [memT])
                wts = [fw.sb("xw%d" % i, [128, KC, 512], BF16) for i in range(2)]
                xwst = fw.sb("xwst", [128, KC, 512])
                mk32 = fw.sb("mk32", [128, 512])
                cnt = 0
                for which, wname, oname in ((0, "wk", "nmk"), (1, "wv", "nmv")):
                    wv_ = I[wname].t.ap()[l].rearrange("(kc p) n -> p kc n", p=128)
                    for n0 in range(0, D, 512):
                        wt = wts[cnt % 2]; cnt += 1
                        fw.dma(xwst[:], wv_[:, :, n0:n0 + 512], [], [xwst])
                        G(lambda e, wt=wt: e.tensor_copy(wt[:], xwst[:]), [xwst], [wt])
                        for mc in range(MC):
                            ps = pA[mc % 4]
                            for kc in range(KC):
                                P(lambda e, ps=ps, kc=kc, wt=wt, mc=mc: e.matmul(ps[:], memT[:, kc, mc * 128:(mc + 1) * 128], wt[:, kc, :], start=(kc == 0), stop=(kc == KC - 1)), [memT, wt], [ps])
                            A(lambda e, ps=ps: e.copy(mk32[:], ps[:]), [ps], [mk32])
                            fw.dma(O[oname][l, mc * 128:(mc + 1) * 128, n0:n0 + 512], mk32[:], [mk32], [O[oname]], par=True, is_out=True)
                            if which == 1:
                                V(lambda e, mc=mc, n0=n0: e.tensor_copy(mvb[0][:, mc, n0:n0 + 512], mk32[:]), [mk32], [mvb[0]])
                            else:
                                pt = pT[mc % 2]
                                for j in range(4):
                                    P(lambda e, pt=pt, j=j: e.transpose(pt[:, j * 128:(j + 1) * 128], mk32[:, j * 128:(j + 1) * 128], ident[:]), [mk32, ident], [pt])
                                kc0 = n0 // 128
                                evac(mkT[0][:, kc0:kc0 + 4, mc * 128:(mc + 1) * 128], pt[:].rearrange("p (a b) -> p a b", a=4), [pt], [mkT[0]])
                for mc in range(MC):
                    fw.dma(mt[:], I["cmk"][l, mc * 128:(mc + 1) * 128, :], [], [mt])
                    for g in range(KC // 4):
                        pt = pT[g % 2]
                        for j in range(4):
                            kc = 4 * g + j
                            P(lambda e, pt=pt, j=j, kc=kc: e.transpose(pt[:, j * 128:(j + 1) * 128], mt[:, kc * 128:(kc + 1) * 128], ident[:]), [mt, ident], [pt])
                        evac(mkT[1][:, 4 * g:4 * g + 4, mc * 128:(mc + 1) * 128], pt[:].rearrange("p (a b) -> p a b", a=4), [pt], [mkT[1]])
                    fw.dma(mt[:], I["cmv"][l, mc * 128:(mc + 1) * 128, :], [], [mt])
                    G(lambda e, mc=mc: e.tensor_copy(mvb[1][:, mc, :], mt[:]), [mt], [mvb[1]])
            Wq = load_wres("Wq", I["wq"].t.ap()[l])
            Wo = Wq
            g_t = load_bcast("lxg", I["lnx_g"][l:l + 1, :], D)
            b_t = load_bcast("lxb", I["lnx_b"][l:l + 1, :], D)
            scr = {"st": fw.sb("xst", [128, 8]), "xo": fw.sb("xxo", [128, D]),
                   "xr": fw.sb("xxr", [128, D]), "tt": fw.sb("xtt", [128, D])}
            xTs = [fw.sb("x_xT%d" % i, [128, KC, 128], BF16) for i in range(2)]
            qTs = [fw.sb("x_qT%d" % i, [128, KC, 128], BF16) for i in range(2)]
            oTs = [fw.sb("x_oT%d" % i, [128, KC, 128], BF16) for i in range(2)]
            prs = [fw.sb("x_p%d" % i, [128, ML]) for i in range(2)]
            pTss = [fw.sb("x_pT%d" % i, [128, MC, 128], BF16) for i in range(2)]
            sms = [fw.sb("x_sm%d" % i, [128, 8]) for i in range(2)]
            xins = [scr["xr"], scr["tt"]]
            for ti in range(NT + 1):
                s = 0 if ti < NT else 1
                r0 = xrow(ti)
                xin = xins[ti % 2]; xT = xTs[ti % 2]; qT = qTs[ti % 2]; oT = oTs[ti % 2]
                fw.dma(xin[:], XR[r0:r0 + 128, :], [XR], [xin])
                transpose_to(xT, xin, KC)
                for g in range(KC // 4):
                    ps = pB[g % 2]
                    for j in range(4):
                        cb = 4 * g + j
                        for kc in range(KC):
                            P(lambda e, ps=ps, j=j, cb=cb, kc=kc, xT=xT: e.matmul(ps[:, j * 128:(j + 1) * 128], Wq[:, kc, cb * 128:(cb + 1) * 128], xT[:, kc, :], start=(kc == 0), stop=(kc == KC - 1)),
                              [Wq, xT], [ps])
                    evac(qT[:, 4 * g:4 * g + 4, :], ps[:].rearrange("p (a b) -> p a b", a=4), [ps], [qT])

                def hA(h, qT=qT, s=s):
                    ps = pA[h % 2]; pr = prs[h % 2]; sm = sms[h % 2]
                    for dc in range(DC):
                        P(lambda e, dc=dc: e.matmul(ps[:, 0:ML], qT[:, h * DC + dc, :], mkT[s][:, h * DC + dc, :], start=(dc == 0), stop=(dc == DC - 1)),
                          [qT, mkT[s]], [ps])
                    V(lambda e: e.reduce_max(sm[:, 0:1], ps[:, 0:ML], AX.X), [ps], [sm])
                    V(lambda e: e.tensor_scalar(sm[:, 1:2], sm[:, 0:1], -scale, None, ALU.mult), [sm], [sm])
                    A(lambda e: e.activation(pr[:], ps[:, 0:ML], AF.Exp, bias=sm[:, 1:2], scale=scale, accum_out=sm[:, 2:3]), [ps, sm], [pr, sm])
                    V(lambda e: e.reciprocal(sm[:, 3:4], sm[:, 2:3]), [sm], [sm])
                    V(lambda e: e.tensor_scalar(pr[:], pr[:], sm[:, 3:4], None, ALU.mult), [pr, sm], [pr])

                def hB(h, oT=oT, s=s):
                    pr = prs[h % 2]; pTs = pTss[h % 2]
                    transpose_to(pTs, pr, MC)
                    po = pB[h % 2]
                    for dc in range(DC):
                        for mc in range(MC):
                            P(lambda e, dc=dc, mc=mc: e.matmul(po[:, dc * 128:(dc + 1) * 128], mvb[s][:, mc, (h * DC + dc) * 128:(h * DC + dc + 1) * 128], pTs[:, mc, :], start=(mc == 0), stop=(mc == MC - 1)),
                              [mvb[s], pTs], [po])
                    evac(oT[:, h * DC:(h + 1) * DC, :], po[:, 0:DC * 128].rearrange("p (a b) -> p a b", a=DC), [po], [oT])

                hA(0)
                for h in range(XH):
                    if h + 1 < XH:
                        hA(h + 1)
                    hB(h)
                fw.dma(OT[ti, :, :], oT[:].rearrange("p a b -> p (a b)"), [oT], [OT], par=True)
            fw.barrier()
            reload_wres(Wq, I["wo"].t.ap()[l])
            for ti in range(NT + 1):
                oT = oTs[ti % 2]
                fw.dma(oT[:].rearrange("p a b -> p (a b)"), OT[ti, :, :], [OT], [oT])
                proj_residual_ln(ti, oT, Wo, g_t, b_t, scr, final)

    def gate_stage_even(e_):
        l = 2 * e_
        with fw.scope():
            Wout = load_wres("Wout", I["w_out_e"].t.ap()[e_])
            g_t = load_bcast("lmg", I["lnm_g"][l:l + 1, :], D)
            b_t = load_bcast("lmb", I["lnm_b"][l:l + 1, :], D)
            scr = {"st": fw.sb("gst", [128, 8]), "xo": fw.sb("gxo", [128, D]),
                   "xr": fw.sb("gxr", [128, D]), "tt": fw.sb("gtt", [128, D])}
            yt = fw.sb("g_y", [128, D]); zt = fw.sb("g_z", [128, D])
            yT = fw.sb("g_yT", [128, KC, 128], BF16)
            zc0 = 3 * AW + c.BCOLS
            for ti in range(NT + 1):
                r0 = xrow(ti); h0 = herow(ti)
                fw.dma(yt[:], Y[r0:r0 + 128, :], [Y], [yt])
                fw.dma(zt[:], HE[h0:h0 + 128, zc0:zc0 + D], [HE], [zt])
                A(lambda e: e.activation(zt[:], zt[:], AF.Silu), [zt], [zt])
                V(lambda e: e.tensor_tensor(yt[:], yt[:], zt[:], ALU.mult), [yt, zt], [yt])
                transpose_to(yT, yt, KC)
                proj_residual_ln(ti, yT, Wout, g_t, b_t, scr)


    def odd_stage(o):
        l = 2 * o + 1
        PGW = c.PGW; JC = KC // 4
        gemm_f1(I["w_in_o"].t.ap()[o], 2 * D, HO, xrow)
        fw.dma(O["npp"][o, :, :], HO[T - 15:T, 0:D], [HO], [O["npp"]], par=True, is_out=True)
        fw.dma(O["nps"][o, 0:11, :], I["spool"][o, 4:15, :], [], [O["nps"]], par=True, is_out=True)
        fw.dma(O["nps"][o, 11:15, :], HO[T:T + 4, 0:D], [HO], [O["nps"]], par=True, is_out=True)
        with fw.scope():
            Wout = load_wres("WoutO", I["w_out_o"].t.ap()[o])
            g_t = load_bcast("lmgo", I["lnm_g"][l:l + 1, :], D)
            b_t = load_bcast("lmbo", I["lnm_b"][l:l + 1, :], D)
            psc = load_bcast("psc", I["pool_scale"][o:o + 1, :], D)
            PW = fw.sb("PW", [128, 4, JC, PGW], BF16)
            with fw.scope():
                pwst = fw.sb("pwst", [128, JC, PGW])
                for g in range(4):
                    fw.dma(pwst[:], I["pool_w"].t.ap()[o, g].rearrange("(j p) n -> p j n", p=128), [], [pwst])
                    G(lambda e, g=g: e.tensor_copy(PW[:, g, :, :], pwst[:]), [pwst], [PW])
            pms = {}
            for nm in ("pm0", "pmg", "pmp"):
                pms[nm] = fw.sb("s_" + nm, [128, 4, 128])
                fw.dma(pms[nm][:], I[nm][:, :, :], [], [pms[nm]])
            scr = {"st": fw.sb("ost", [128, 8]), "xo": fw.sb("oxo", [128, D]),
                   "xr": fw.sb("oxr", [128, D]), "tt": fw.sb("ott", [128, D])}
            xcs = [fw.sb("o_xc%d" % i, [128, D]) for i in range(2)]
            xsp = scr["tt"]
            zt = fw.sb("o_z", [128, D]); yt = fw.sb("o_y", [128, D])
            pTb = fw.sb("o_pT", [128, KC, 128], BF16)
            yT = fw.sb("o_yT", [128, KC, 128], BF16)
            for ti in range(NT + 1):
                r0 = xrow(ti)
                xc = xcs[ti % 2]
                fw.dma(xc[:], HO[r0:r0 + 128, 0:D], [HO], [xc])
                fw.dma(zt[:], HO[r0:r0 + 128, D:2 * D], [HO], [zt])
                if ti == 0:
                    pm, prev = pms["pm0"], None
                elif ti < NT:
                    pm, prev = pms["pmg"], xcs[(ti - 1) % 2]
                else:
                    pm, prev = pms["pmg"], xsp
                    V(lambda e: e.memset(xsp[:], 0.0), [], [xsp])
                    fw.dma(xsp[113:128, :], I["spool"][o, :, :], [], [xsp])
                for g4 in range(KC // 4):
                    pt = pT[g4 % 2]
                    for j in range(4):
                        cb = 4 * g4 + j
                        g = cb // JC
                        P(lambda e, pt=pt, j=j, cb=cb, g=g, xc=xc, pm=pm, prev=prev: e.matmul(pt[:, j * 128:(j + 1) * 128], xc[:, cb * 128:(cb + 1) * 128], pm[:, g, :], start=True, stop=(prev is None)),
                          [xc, pm], [pt])
                        if prev is not None:
                            P(lambda e, pt=pt, j=j, cb=cb, g=g, prev=prev: e.matmul(pt[:, j * 128:(j + 1) * 128], prev[:, cb * 128:(cb + 1) * 128], pms["pmp"][:, g, :], start=False, stop=True),
                              [prev, pms["pmp"]], [pt])
                    evac(pTb[:, 4 * g4:4 * g4 + 4, :], pt[:].rearrange("p (a b) -> p a b", a=4), [pt], [pTb])
                A(lambda e: e.activation(zt[:], zt[:], AF.Silu), [zt], [zt])
                G(lambda e: e.tensor_tensor(zt[:], zt[:], psc[:], ALU.mult), [zt, psc], [zt])
                for g in range(4):
                    ps = pA[g]
                    for j in range(JC):
                        P(lambda e, ps=ps, g=g, j=j: e.matmul(ps[:, 0:PGW], pTb[:, g * JC + j, :], PW[:, g, j, :], start=(j == 0), stop=(j == JC - 1)), [pTb, PW], [ps])
                    V(lambda e, ps=ps, g=g: e.tensor_tensor(yt[:, g * PGW:(g + 1) * PGW], ps[:, 0:PGW], zt[:, g * PGW:(g + 1) * PGW], ALU.mult), [ps, zt], [yt])
                transpose_to(yT, yt, KC)
                proj_residual_ln(ti, yT, Wout, g_t, b_t, scr)


    def t5_tiles():
        tbs = fw.sb("tbs", [128, AH, 256])
        c31 = fw.sb("c31", [128, AH])
        ncmax = fw.sb("ncmax", [128, AH])
        with fw.scope():
            t5s = fw.sb("t5s", [32, AH]); E = fw.sb("t5Es", [32, 384]); pad = fw.sb("t5pads", [128, 384])
            t5b = fw.sb("t5b", [128, 32 * AH]); rep = fw.sb("t5rep", [32, 128]); zt = fw.sb("t5zt", [128, 384])
            fw.dma(t5s[:], I["t5"][:, :], [], [t5s])
            fw.dma(E[:], I["t5E"][:, :], [], [E])
            fw.dma(pad[:], I["t5pad"][:, :], [], [pad])
            fw.dma(c31[:], I["t5"][31:32, :].partition_broadcast(128), [], [c31])
            fw.dma(t5b[:], I["t5"].t.ap().rearrange("b h -> (b h)").rearrange("(o n) -> o n", o=1).partition_broadcast(128), [], [t5b])
            V(lambda e: e.tensor_reduce(ncmax[:], t5b[:].rearrange("p (b h) -> p h b", h=AH), AX.X, ALU.max), [t5b], [ncmax])
            V(lambda e: e.tensor_scalar(ncmax[:], ncmax[:], -1.0, None, ALU.mult), [ncmax], [ncmax])
            for h in range(AH):
                V(lambda e, h=h: e.tensor_copy(rep[:], t5s[:, h:h + 1].to_broadcast([32, 128])), [t5s], [rep])
                P(lambda e: e.matmul(pB[0][:, 0:384], rep[:], E[:], start=True, stop=True), [rep, E], [pB[0]])
                V(lambda e: e.tensor_tensor(zt[:], pB[0][:, 0:384], pad[:], ALU.add), [pB[0], pad], [zt])
                fw.dma(TZ[h, :, :], zt[:], [zt], [TZ])
                src = bass.AP(TZ.t, h * 128 * 384 + 127, [[383, 128], [1, 256]])
                fw.dma(tbs[:, h, :], src, [TZ], [tbs])
        return tbs, c31, ncmax

    T5C = {}

    def moba_prompt_stage(e_):
        scale = 1.0 / math.sqrt(128.0)
        NBLK = T // 256
        if "t" not in T5C:
            T5C["t"] = t5_tiles()
        tbs, c31, ncmax = T5C["t"]
        if chk('A0'):
            return
        with fw.scope():
            qtm = fw.sb("a_qtm", [128, NT * 128]); ktm = fw.sb("a_ktm", [128, NT * 128]); vtm = fw.sb("a_vtm", [128, NT * 128])
            QT = fw.sb("a_QT", [128, NT, 128], BF16); KT = fw.sb("a_KT", [128, NT, 128], BF16); Vb = fw.sb("a_Vb", [128, NT, 128], BF16)
            km = fw.sb("a_km", [128, 8]); kmb = fw.sb("a_kmb", [128, 8], BF16)
            gs = fw.sb("a_gs", [128, 8]); m8 = fw.sb("a_m8", [128, 8]); mb = fw.sb("a_mb", [128, 8])
            fb = fw.sb("a_fb", [128, 8]); nbm = fw.sb("a_nbm", [128, 8]); sm = fw.sb("a_sm", [128, 8]); negm = fw.sb("a_negm", [128, 4])
            rs = fw.sb("a_rs", [128, NT]); lt = fw.sb("a_lt", [128, 128])
            Pm = fw.sb("a_P", [128, NT * 128]); PT = fw.sb("a_PT", [128, NT, 128], BF16)
            AOh = fw.sb("a_AO", [128, NT, 128])
            for h in range(AH):
                for (dst, c0) in ((qtm, h * 128), (ktm, AW + h * 128), (vtm, 2 * AW + h * 128)):
                    fw.dma(dst[:].rearrange("p (t d) -> p t d", d=128), HE[1:1 + T, c0:c0 + 128].rearrange("(t p) d -> p t d", p=128), [HE], [dst])
                transpose_to(QT, qtm, NT)
                transpose_to(KT, ktm, NT)
                G(lambda e: e.tensor_copy(Vb[:].rearrange("p t d -> p (t d)"), vtm[:]), [vtm], [Vb])
                V(lambda e: e.memset(km[:], 0.0), [], [km])
                V(lambda e: e.tensor_reduce(km[:, 0:NBLK], KT[:].rearrange("p (b u) k -> p b (u k)", u=2), AX.X, ALU.add), [KT], [km])
                V(lambda e: e.tensor_copy(kmb[:], km[:]), [km], [kmb])
                if chk('A1'):
                    return
                for qt in range(NT):
                    i = qt // 2
                    nk = (qt + 1) * 128
                    ncg = (nk + 511) // 512
                    P(lambda e, qt=qt: e.matmul(pB[0][:, 0:8], QT[:, qt, :], kmb[:], start=True, stop=True), [QT, kmb], [pB[0]])
                    if i >= 4:
                        V(lambda e: e.memset(gs[:], -1e30), [], [gs])
                        V(lambda e, i=i: e.tensor_copy(gs[:, 0:i], pB[0][:, 0:i]), [pB[0]], [gs])
                        V(lambda e: e.max(m8[:], gs[:]), [gs], [m8])
                        V(lambda e: e.tensor_scalar(mb[:], gs[:], m8[:, 2:3], None, ALU.is_ge), [gs, m8], [mb])
                        V(lambda e: e.tensor_scalar(mb[:], mb[:], -1.0, 30000.0, ALU.add, ALU.mult), [mb], [mb])
                    else:
                        V(lambda e: e.memset(mb[:], 0.0), [], [mb])
                    if chk('A2') and qt == NT - 1:
                        return
                    for cg in range(ncg):
                        w = min(512, nk - cg * 512)
                        P(lambda e, cg=cg, w=w, qt=qt: e.matmul(pA[cg][:, 0:w], QT[:, qt, :], KT[:, 4 * cg:4 * cg + w // 128, :].rearrange("p t k -> p (t k)"), start=True, stop=True),
                          [QT, KT], [pA[cg]])
                        V(lambda e, cg=cg, w=w: e.reduce_max(sm[:, cg:cg + 1], pA[cg][:, 0:w], AX.X), [pA[cg]], [sm])
                    V(lambda e, ncg=ncg: e.reduce_max(negm[:, 0:1], sm[:, 0:ncg], AX.X), [sm], [negm])
                    V(lambda e, h=h: e.tensor_scalar(negm[:, 1:2], negm[:, 0:1], -scale, ncmax[:, h:h + 1], ALU.mult, ALU.add), [negm, ncmax], [negm])
                    V(lambda e, h=h: e.tensor_scalar(fb[:], mb[:], negm[:, 1:2], c31[:, h:h + 1], ALU.add, ALU.add), [mb, negm, c31], [fb])
                    V(lambda e: e.tensor_scalar(nbm[:], mb[:], negm[:, 1:2], None, ALU.add), [mb, negm], [nbm])
                    for kt in range(qt + 1):
                        j = kt // 2; cg = kt // 4; off = (kt % 4) * 128
                        if kt >= qt - 1:
                            tc0 = (kt - (qt - 1)) * 128
                            V(lambda e, cg=cg, off=off, tc0=tc0, h=h: e.scalar_tensor_tensor(lt[:], pA[cg][:, off:off + 128], scale, tbs[:, h, tc0:tc0 + 128], ALU.mult, ALU.add),
                              [pA[cg], tbs], [lt])
                            bias = nbm[:, j:j + 1] if j < i else negm[:, 1:2]
                            A(lambda e, kt=kt, bias=bias: e.activation(Pm[:, kt * 128:(kt + 1) * 128], lt[:], AF.Exp, bias=bias, scale=1.0, accum_out=rs[:, kt:kt + 1]),
                              [lt, nbm, negm], [Pm, rs])
                        else:
                            A(lambda e, kt=kt, cg=cg, off=off, j=j: e.activation(Pm[:, kt * 128:(kt + 1) * 128], pA[cg][:, off:off + 128], AF.Exp, bias=fb[:, j:j + 1], scale=scale, accum_out=rs[:, kt:kt + 1]),
                              [pA[cg], fb], [Pm, rs])
                    if chk('A3') and qt == NT - 1:
                        return
                    V(lambda e, qt=qt: e.reduce_sum(sm[:, 4:5], rs[:, 0:qt + 1], AX.X), [rs], [sm])
                    V(lambda e: e.reciprocal(sm[:, 5:6], sm[:, 4:5]), [sm], [sm])
                    transpose_to(PT, Pm, qt + 1)
                    for kt in range(qt + 1):
                        P(lambda e, kt=kt, qt=qt: e.matmul(pB[1][:, 0:128], PT[:, kt, :], Vb[:, kt, :], start=(kt == 0), stop=(kt == qt)), [PT, Vb], [pB[1]])
                    V(lambda e, qt=qt: e.tensor_scalar(AOh[:, qt, :], pB[1][:, 0:128], sm[:, 5:6], None, ALU.mult), [pB[1], sm], [AOh])
                fw.dma(Y[0:T, h * 128:(h + 1) * 128].rearrange("(t p) d -> p t d", p=128), AOh[:], [AOh], [Y], par=True)


    def rwkv_stage(e_):
        HP = BH // 2
        xb0 = 3 * AW
        NBK = (BW + 511) // 512
        with fw.scope():
            cur = fw.sb("r_cur", [128, c.BCOLS]); big = fw.sb("r_big", [128, c.BCOLS])
            cA = fw.sb("r_cA", [128, BW]); cB = fw.sb("r_cB", [128, BW])
            mu_t = load_bcast("r_mu", I["mu"][e_:e_ + 1, :], c.BCOLS)
            wl = fw.sb("r_wl", [128, BW]); av = fw.sb("r_a", [128, BW]); kkn = fw.sb("r_kkn", [128, BW])
            kmod = fw.sb("r_kmod", [128, BW]); beta = fw.sb("r_beta", [128, BW]); T4 = fw.sb("r_T4", [128, BW])
            T1 = Buf(big.t[:, 0:BW], "T1v"); T2 = Buf(big.t[:, BW:2 * BW], "T2v"); T3 = Buf(big.t[:, 2 * BW:3 * BW], "T3v")
            for tv in (T1, T2, T3):
                tv.par = big
            RbT = fw.sb("r_RbT", [128, HP, 128]); KtT = fw.sb("r_KtT", [128, HP, 128]); BtT = fw.sb("r_BtT", [128, HP, 128]); AbT = fw.sb("r_AbT", [128, HP, 128])
            MT = fw.sb("r_MT", [128, BH, 384])
            Xs = [fw.sb("r_X%d" % i, [128, BH, 128]) for i in range(2)]
            Zs = [fw.sb("r_Z%d" % i, [128, BH, 128]) for i in range(2)]
            Pm = fw.sb("r_P", [128, BH, 128])
            Yo = fw.sb("r_Yo", [128, BW]); W0s = fw.sb("r_W0", [128, BW]); Us = fw.sb("r_U", [128, BW])
            Hst = [fw.sb("r_H%d" % i, [128, HP, 64]) for i in range(2)]
            gam = fw.sb("r_gam", [128, HP, 2])
            Hm = [fw.sb("r_Hm%d" % i, [128, HP, 64]) for i in range(2)]
            lT = fw.sb("r_lT", [128, 128]); lup = fw.sb("r_lup", [128, BW])
            msk = fw.sb("r_msk", [128, 640]); tri = fw.sb("r_tri", [128, 128]); suf = fw.sb("r_suf", [128, 128]); cind = fw.sb("r_cind", [128, 2])
            st = fw.sb("r_st", [128, 8 * BH]); rowm = fw.sb("r_rowm", [128, 1]); iop = fw.sb("r_iop", [128, 1])
            So = fw.sb("r_So", [64, BH, 64])
            fw.dma(msk[:], I["rw_mask"][:, :], [], [msk]); fw.dma(tri[:], I["tri_incl"][:, :], [], [tri])
            fw.dma(suf[:], I["tri_suf"][:, :], [], [suf]); fw.dma(cind[:], I["chunk_ind"][:, :], [], [cind])
            fw.dma(iop[:], I["iota_p"][:, :], [], [iop])
            V(lambda e: e.tensor_scalar(rowm[:], iop[:], float(NS), None, ALU.is_lt), [iop], [rowm])
            fw.dma(lup[0:64, :], I["w_up"][e_, :, :], [], [lup])
            fw.dma(lup[64:128, :], I["a_up"][e_, :, :], [], [lup], par=True)
            V(lambda e: e.memset(Hst[0][:], 0.0), [], [Hst[0]])
            fw.dma(So[:], I["srw"].t.ap()[e_].rearrange("(h v) k -> v h k", v=64), [], [So])
            for hp in range(HP):
                P(lambda e, hp=hp: e.transpose(pT[0][:, hp * 64:(hp + 1) * 64], So[:, 2 * hp:2 * hp + 2, :].rearrange("v h k -> v (h k)"), ident[0:64, 0:64]), [So, ident], [pT[0]])
            V(lambda e: e.tensor_copy(Hst[1][:], pT[0][:, 0:HP * 64].rearrange("p (a b) -> p a b", b=64)), [pT[0]], [Hst[1]])

            if chk('Ri'):
                return

            def cload(buf, name):
                fw.dma(buf[:], I[name][e_:e_ + 1, :].partition_broadcast(128), [], [buf])

            def hv(t):
                return t[:].rearrange("p (h k) -> p h k", k=64)

            def hb(t2):
                return t2.rearrange("p (h o) -> p h o", o=1).to_broadcast([128, BH, 64])

            for ti in range(NT + 1):
                sample = (ti == NT)
                H = Hst[1] if sample else Hst[0]
                h0 = herow(ti)
                fw.dma(cur[:], HE[h0:h0 + 128, xb0:xb0 + c.BCOLS], [HE], [cur])
                fw.dma(big[:], HE[h0 - 1:h0 + 127, xb0:xb0 + c.BCOLS], [HE], [big])
                V(lambda e: e.tensor_tensor(big[:], big[:], cur[:], ALU.subtract), [big, cur], [big])
                G(lambda e: e.tensor_tensor(big[:], big[:], mu_t[:], ALU.mult), [big, mu_t], [big])
                V(lambda e: e.tensor_tensor(cur[:], cur[:], big[:], ALU.add), [cur, big], [cur])
                rv = cur.t[:, 0:BW]; kv = cur.t[:, BW:2 * BW]; vv = cur.t[:, 2 * BW:3 * BW]
                if chk('R0'):
                    return
                P(lambda e: e.transpose(pT[0][:, 0:128], cur[:, 3 * BW:3 * BW + 128], ident[:]), [cur, ident], [pT[0]])
                A(lambda e: e.activation(lT[0:64, :], pT[0][0:64, 0:128], AF.Tanh), [pT[0]], [lT])
                V(lambda e: e.tensor_copy(lT[64:128, :], pT[0][64:128, 0:128]), [pT[0]], [lT])
                cload(cA, "w0"); cload(cB, "a0")
                for nb in range(NBK):
                    w = min(512, BW - nb * 512)
                    P(lambda e, nb=nb, w=w: e.matmul(pA[nb][:, 0:w], lT[0:64, :], lup[0:64, nb * 512:nb * 512 + w], start=True, stop=True), [lT, lup], [pA[nb]])
                    P(lambda e, nb=nb, w=w: e.matmul(pA[2 + nb][:, 0:w], lT[64:128, :], lup[64:128, nb * 512:nb * 512 + w], start=True, stop=True), [lT, lup], [pA[2 + nb]])
                    V(lambda e, nb=nb, w=w: e.tensor_tensor(wl[:, nb * 512:nb * 512 + w], pA[nb][:, 0:w], cA[:, nb * 512:nb * 512 + w], ALU.add), [pA[nb], cA], [wl])
                    V(lambda e, nb=nb, w=w: e.tensor_tensor(av[:, nb * 512:nb * 512 + w], pA[2 + nb][:, 0:w], cB[:, nb * 512:nb * 512 + w], ALU.add), [pA[2 + nb], cB], [av])
                A(lambda e: e.activation(wl[:], wl[:], AF.Sigmoid), [wl], [wl])
                A(lambda e: e.activation(av[:], av[:], AF.Sigmoid), [av], [av])
                if sample:
                    V(lambda e: e.tensor_scalar(wl[:], wl[:], -0.6065306597126334, rowm[:, 0:1], ALU.mult, ALU.mult), [wl, rowm], [wl])
                else:
                    V(lambda e: e.tensor_scalar(wl[:], wl[:], -0.6065306597126334, None, ALU.mult), [wl], [wl])
                if chk('R1'):
                    return
                cload(cA, "k_k"); cload(cB, "k_a")
                V(lambda e: e.tensor_tensor(kkn[:], kv, cA[:], ALU.mult), [cur, cA], [kkn])
                G(lambda e: e.tensor_tensor(T4[:], kkn[:], kkn[:], ALU.mult), [kkn], [T4])
                V(lambda e: e.tensor_reduce(st[:, 0:BH], hv(T4), AX.X, ALU.add), [T4], [st])
                A(lambda e: e.activation(st[:, 0:BH], st[:, 0:BH], AF.Sqrt), [st], [st])
                V(lambda e: e.tensor_scalar(st[:, 0:BH], st[:, 0:BH], 1e-12, None, ALU.max), [st], [st])
                V(lambda e: e.reciprocal(st[:, BH:2 * BH], st[:, 0:BH]), [st], [st])
                V(lambda e: e.tensor_tensor(hv(kkn), hv(kkn), hb(st[:, BH:2 * BH]), ALU.mult), [kkn, st], [kkn])
                V(lambda e: e.scalar_tensor_tensor(T4[:], av[:], -1.0, cB[:], ALU.add, ALU.mult), [av, cB], [T4])
                V(lambda e: e.scalar_tensor_tensor(kmod[:], T4[:], 1.0, kv, ALU.add, ALU.mult), [T4, cur], [kmod])
                G(lambda e: e.tensor_tensor(beta[:], kkn[:], av[:], ALU.mult), [kkn, av], [beta])
                if sample:
                    V(lambda e: e.tensor_scalar(kmod[:], kmod[:], rowm[:, 0:1], None, ALU.mult), [kmod, rowm], [kmod])
                    V(lambda e: e.tensor_scalar(beta[:], beta[:], rowm[:, 0:1], None, ALU.mult), [beta, rowm], [beta])
                cload(cA, "r_k")
                G(lambda e: e.tensor_tensor(T4[:], rv, kmod[:], ALU.mult), [cur, kmod], [T4])
                V(lambda e: e.tensor_tensor(T4[:], T4[:], cA[:], ALU.mult), [T4, cA], [T4])
                V(lambda e: e.tensor_reduce(st[:, 2 * BH:3 * BH], hv(T4), AX.X, ALU.add), [T4], [st])
                if chk('R2'):
                    return
                for nb in range(NBK):
                    w = min(512, BW - nb * 512)
                    sl = slice(nb * 512, nb * 512 + w)
                    P(lambda e, nb=nb, w=w, sl=sl: e.matmul(pA[nb][:, 0:w], tri[:], wl[:, sl], start=True, stop=True), [tri, wl], [pA[nb]])
                    P(lambda e, nb=nb, w=w, sl=sl: e.matmul(pA[2 + nb][:, 0:w], suf[:], wl[:, sl], start=True, stop=True), [suf, wl], [pA[2 + nb]])
                    A(lambda e, nb=nb, w=w, sl=sl: e.activation(T1[:, sl], pA[nb][:, 0:w], AF.Exp), [pA[nb]], [big])
                    A(lambda e, nb=nb, w=w, sl=sl: e.activation(T2[:, sl], pA[nb][:, 0:w], AF.Exp, scale=-1.0), [pA[nb]], [big])
                    V(lambda e, nb=nb, w=w, sl=sl: e.tensor_tensor(T4[:, sl], pA[nb][:, 0:w], wl[:, sl], ALU.subtract), [pA[nb], wl], [T4])
                A(lambda e: e.activation(T4[:], T4[:], AF.Exp), [T4], [T4])
                V(lambda e: e.scalar_tensor_tensor(T4[:], T4[:], -1.0, kkn[:], ALU.mult, ALU.mult), [T4, kkn], [T4])
                G(lambda e: e.tensor_tensor(T3[:], T2[:], beta[:], ALU.mult), [big, beta], [big])
                V(lambda e: e.tensor_tensor(T2[:], T2[:], kmod[:], ALU.mult), [big, kmod], [big])
                V(lambda e: e.tensor_tensor(T1[:], T1[:], rv, ALU.mult), [big, cur], [big])
                if chk('R3'):
                    return
                for hp in range(HP):
                    P(lambda e, hp=hp: e.matmul(pB[0][:, 2 * hp:2 * hp + 2], wl[:, hp * 128:(hp + 1) * 128], cind[:], start=True, stop=True), [wl, cind], [pB[0]])
                A(lambda e: e.activation(gam[:].rearrange("p a b -> p (a b)"), pB[0][:, 0:2 * HP], AF.Exp), [pB[0]], [gam])
                transpose_to(RbT, T1, HP); transpose_to(KtT, T2, HP); transpose_to(BtT, T3, HP); transpose_to(AbT, T4, HP)
                for nb in range(NBK):
                    w = min(512, BW - nb * 512)
                    sl = slice(nb * 512, nb * 512 + w)
                    A(lambda e, nb=nb, w=w, sl=sl: e.activation(T3[:, sl], pA[2 + nb][:, 0:w], AF.Exp), [pA[2 + nb]], [big])
                V(lambda e: e.tensor_tensor(T1[:], T3[:], kmod[:], ALU.mult), [big, kmod], [big])
                G(lambda e: e.tensor_tensor(T2[:], T3[:], beta[:], ALU.mult), [big, beta], [big])
                Kh, Bh = T1, T2
                if chk('R4'):
                    return
                for h in range(BH):
                    hp, pb = h // 2, (h % 2) * 64
                    ps = pA[h % 2]; pz = pB[h % 2]
                    ops = ((KtT, RbT), (KtT, AbT), (BtT, RbT), (BtT, AbT))
                    for j, (lt_, rt_) in enumerate(ops):
                        P(lambda e, ps=ps, j=j, lt_=lt_, rt_=rt_, hp=hp, pb=pb: e.matmul(ps[:, j * 128:(j + 1) * 128], lt_[pb:pb + 64, hp, :], rt_[pb:pb + 64, hp, :], start=True, stop=True), [lt_, rt_], [ps])
                    P(lambda e, pz=pz, hp=hp, pb=pb: e.matmul(pz[:, 0:128], AbT[pb:pb + 64, hp, :], BtT[pb:pb + 64, hp, :], start=True, stop=True), [AbT, BtT], [pz])
                    V(lambda e, ps=ps, h=h: e.tensor_tensor(MT[:, h, :], ps[:, 0:384], msk[:, 0:384], ALU.mult), [ps, msk], [MT])
                    G_or_V = V
                    V(lambda e, ps=ps, h=h: e.tensor_tensor(Xs[0][:, h, :], ps[:, 384:512], msk[:, 384:512], ALU.mult), [ps, msk], [Xs[0]])
                    V(lambda e, pz=pz, h=h: e.tensor_tensor(Zs[0][:, h, :], pz[:, 0:128], msk[:, 512:640], ALU.mult), [pz, msk], [Zs[0]])
                if chk('R5'):
                    return
                V(lambda e: e.tensor_tensor(Pm[:], Xs[0][:], ident[:].rearrange("p (o k) -> p o k", o=1).to_broadcast([128, BH, 128]), ALU.add), [Xs[0], ident], [Pm])
                if chk('R5a'):
                    return
                cu = 0
                for n in range(1, 1 + int(_os.environ.get('KDBL', '5'))):
                    Xp, Zp = Xs[cu], Zs[cu]; Xn, Zn = Xs[1 - cu], Zs[1 - cu]
                    for g0 in range(0, BH, 4):
                        ng = min(4, BH - g0)
                        pz = pA[(g0 // 4) % 2]
                        for j in range(ng):
                            P(lambda e, pz=pz, j=j, h=g0 + j, Xp=Xp, Zp=Zp: e.matmul(pz[:, j * 128:(j + 1) * 128], Xp[:, h, :], Zp[:, h, :], start=True, stop=True), [Xp, Zp], [pz])
                        evac(Zn[:, g0:g0 + ng, :], pz[:, 0:ng * 128].rearrange("p (a b) -> p a b", b=128), [pz], [Zn])
                        if chk('R5z'):
                            continue
                        if n <= 4:
                            px = pA[2 + (g0 // 4) % 2]
                            for j in range(ng):
                                P(lambda e, px=px, j=j, h=g0 + j, Xp=Xp, Zp=Zp: e.matmul(px[:, j * 128:(j + 1) * 128], Zp[:, h, :], Xp[:, h, :], start=True, stop=True), [Xp, Zp], [px])
                            evac(Xn[:, g0:g0 + ng, :], px[:, 0:ng * 128].rearrange("p (a b) -> p a b", b=128), [px], [Xn])
                        if chk('R5x'):
                            continue
                        pp = pB[(g0 // 4) % 2]
                        for j in range(ng):
                            P(lambda e, pp=pp, j=j, h=g0 + j, Zn=Zn: e.matmul(pp[:, j * 128:(j + 1) * 128], Zn[:, h, :], Pm[:, h, :], start=True, stop=True), [Zn, Pm], [pp])
                        V(lambda e, pp=pp, g0=g0, ng=ng: e.tensor_tensor(Pm[:, g0:g0 + ng, :], Pm[:, g0:g0 + ng, :], pp[:, 0:ng * 128].rearrange("p (a b) -> p a b", b=128), ALU.add), [Pm, pp], [Pm])
                    cu = 1 - cu
                if chk('R6'):
                    return
                NBANK = (BH * 64 + 511) // 512

                def reg(banks, h):
                    return banks[(h * 64) // 512][:, (h * 64) % 512:(h * 64) % 512 + 64]

                for ci in range(2):
                    cs = slice(ci * 64, (ci + 1) * 64)
                    bw_ = [pA[0], pA[1]]; bu_ = [pA[2], pA[3]]; by_ = [pT[0], pT[1]]; bh_ = [pB[0], pB[1]]
                    for par in range(2):
                        V(lambda e, par=par: e.tensor_scalar(Hm[par][:], H[:], cind[:, par:par + 1], None, ALU.mult), [H, cind], [Hm[par]])
                    G(lambda e, ci=ci: e.tensor_scalar(T4[:], cur[:, 2 * BW:3 * BW], cind[:, ci:ci + 1], None, ALU.mult), [cur, cind], [T4])
                    for h in range(BH):
                        hp, par = h // 2, h % 2
                        bk = bw_[(h * 64) // 512]
                        P(lambda e, h=h, hp=hp, par=par: e.matmul(reg(bw_, h), AbT[:, hp, :], Hm[par][:, hp, :], start=True, stop=False), [AbT, Hm[par]], [bk])
                        P(lambda e, h=h: e.matmul(reg(bw_, h), MT[:, h, 128:256], cur[:, 2 * BW + h * 64:2 * BW + (h + 1) * 64], start=False, stop=True), [MT, cur], [bk])
                    for b in range(NBANK):
                        w = min(512, BH * 64 - b * 512)
                        evac(W0s[:, b * 512:b * 512 + w], bw_[b][:, 0:w], [bw_[b]], [W0s])
                    for h in range(BH):
                        bk = bu_[(h * 64) // 512]
                        P(lambda e, h=h: e.matmul(reg(bu_, h), Pm[:, h, :], W0s[:, h * 64:(h + 1) * 64], start=True, stop=True), [Pm, W0s], [bk])
                    for b in range(NBANK):
                        w = min(512, BH * 64 - b * 512)
                        evac(Us[:, b * 512:b * 512 + w], bu_[b][:, 0:w], [bu_[b]], [Us])
                    V(lambda e, ci=ci: e.tensor_scalar(W0s[:], Us[:], cind[:, ci:ci + 1], None, ALU.mult), [Us, cind], [W0s])
                    for h in range(BH):
                        hp, par = h // 2, h % 2
                        bk = by_[(h * 64) // 512]
                        P(lambda e, h=h, hp=hp, par=par: e.matmul(reg(by_, h), RbT[:, hp, :], Hm[par][:, hp, :], start=True, stop=False), [RbT, Hm[par]], [bk])
                        P(lambda e, h=h: e.matmul(reg(by_, h), MT[:, h, 256:384], Us[:, h * 64:(h + 1) * 64], start=False, stop=False), [MT, Us], [bk])
                        P(lambda e, h=h: e.matmul(reg(by_, h), MT[:, h, 0:128], cur[:, 2 * BW + h * 64:2 * BW + (h + 1) * 64], start=False, stop=True), [MT, cur], [bk])
                        bk2 = bh_[(h * 64) // 512]
                        P(lambda e, h=h, hp=hp: e.matmul(reg(bh_, h), Bh[:, hp * 128:(hp + 1) * 128], W0s[:, h * 64:(h + 1) * 64], start=True, stop=False), [big, W0s], [bk2])
                        P(lambda e, h=h, hp=hp: e.matmul(reg(bh_, h), Kh[:, hp * 128:(hp + 1) * 128], T4[:, h * 64:(h + 1) * 64], start=False, stop=True), [big, T4], [bk2])
                    for b in range(NBANK):
                        w = min(512, BH * 64 - b * 512)
                        evac(Yo[cs, b * 512:b * 512 + w], by_[b][cs, 0:w], [by_[b]], [Yo])
                    for b in range(NBANK):
                        w = min(512, BH * 64 - b * 512)
                        nh = w // 128
                        for par in range(2):
                            pb = par * 64
                            hsl = H[pb:pb + 64, b * 4:b * 4 + nh, :]
                            V(lambda e, hsl=hsl, pb=pb, b=b, nh=nh, ci=ci: e.tensor_tensor(hsl, hsl, gam[pb:pb + 64, b * 4:b * 4 + nh, ci:ci + 1].to_broadcast([64, nh, 64]), ALU.mult), [H, gam], [H])
                            V(lambda e, hsl=hsl, pb=pb, b=b, nh=nh, par=par, w=w: e.tensor_tensor(hsl, hsl, bh_[b][pb:pb + 64, 0:w].rearrange("p (q two v) -> p q two v", two=2, v=64)[:, :, par, :], ALU.add), [H, bh_[b]], [H])
                if chk('R7'):
                    return
                cload(cA, "gn_g"); cload(cB, "gn_b")
                V(lambda e: e.tensor_reduce(st[:, 3 * BH:4 * BH], hv(Yo), AX.X, ALU.add), [Yo], [st])
                G(lambda e: e.tensor_tensor(T4[:], Yo[:], Yo[:], ALU.mult), [Yo], [T4])
                V(lambda e: e.tensor_reduce(st[:, 4 * BH:5 * BH], hv(T4), AX.X, ALU.add), [T4], [st])
                V(lambda e: e.tensor_scalar(st[:, 3 * BH:4 * BH], st[:, 3 * BH:4 * BH], 1.0 / 64, None, ALU.mult), [st], [st])
                V(lambda e: e.tensor_tensor(st[:, 5 * BH:6 * BH], st[:, 3 * BH:4 * BH], st[:, 3 * BH:4 * BH], ALU.mult), [st], [st])
                V(lambda e: e.scalar_tensor_tensor(st[:, 4 * BH:5 * BH], st[:, 4 * BH:5 * BH], 1.0 / 64, st[:, 5 * BH:6 * BH], ALU.mult, ALU.subtract), [st], [st])
                A(lambda e: e.activation(st[:, 4 * BH:5 * BH], st[:, 4 * BH:5 * BH], AF.Sqrt, bias=64e-5, scale=1.0), [st], [st])
                V(lambda e: e.reciprocal(st[:, 5 * BH:6 * BH], st[:, 4 * BH:5 * BH]), [st], [st])
                V(lambda e: e.tensor_tensor(hv(Yo), hv(Yo), hb(st[:, 3 * BH:4 * BH]), ALU.subtract), [Yo, st], [Yo])
                V(lambda e: e.tensor_tensor(hv(Yo), hv(Yo), hb(st[:, 5 * BH:6 * BH]), ALU.mult), [Yo, st], [Yo])
                G(lambda e: e.tensor_tensor(Yo[:], Yo[:], cA[:], ALU.mult), [Yo, cA], [Yo])
                G(lambda e: e.tensor_tensor(Yo[:], Yo[:], cB[:], ALU.add), [Yo, cB], [Yo])
                V(lambda e: e.tensor_tensor(hv(T4), cur[:, 2 * BW:3 * BW].rearrange("p (h k) -> p h k", k=64), hb(st[:, 2 * BH:3 * BH]), ALU.mult), [cur, st], [T4])
                V(lambda e: e.tensor_tensor(Yo[:], Yo[:], T4[:], ALU.add), [Yo, T4], [Yo])
                r0 = xrow(ti)
                fw.dma(Y[r0:r0 + 128, AW:AW + BW], Yo[:], [Yo], [Y], par=True)
                if ti == NT - 1 or sample:
                    for hp in range(HP):
                        P(lambda e, hp=hp: e.transpose(pT[0][0:64, (hp % 4) * 128:(hp % 4 + 1) * 128], H[:, hp, :], ident[:]), [H, ident], [pT[0]])
                        if hp % 4 == 3 or hp == HP - 1:
                            g0 = (hp // 4) * 4
                            V(lambda e, g0=g0, hp=hp: e.tensor_copy(So[:, 2 * g0:2 * hp + 2, :].rearrange("v h k -> v (h k)"), pT[0][0:64, 0:(hp - g0 + 1) * 128]), [pT[0]], [So])
                    oname = "nrs" if sample else "nrp"
                    fw.dma(O[oname].t.ap()[e_].rearrange("(h v) k -> v h k", v=64), So[:], [So], [O[oname]], par=True, is_out=True)


    def moba_sample_stage(e_):
        scale = 1.0 / math.sqrt(128.0)
        NPG, NB, PAST = c.NPG, c.NB, c.PAST
        NQ = AH * NS
        if "t" not in T5C:
            T5C["t"] = t5_tiles()
        tbs, c31, ncmax = T5C["t"]
        NAB = (AW + 511) // 512
        with fw.scope():
            KG = dscr("KG%d" % e_, [NPG, 128 * AW]); VG = dscr("VG%d" % e_, [NPG, 128 * AW])
            with fw.scope():
                W = 16 * AW
                CH = 128 * AW // W
                pidx = fw.sb("s_pidx", [NPG, 1], I32); pif = fw.sb("s_pif", [NPG, 1]); jrow = fw.sb("s_jrow", [NPG, CH])
                idxs = fw.sb("s_idxs", [NPG, CH], I32)
                fw.dma(pidx[:], I["ptab"].t.ap().rearrange("o n -> n o"), [], [pidx])
                V(lambda e: e.tensor_copy(pif[:], pidx[:]), [pidx], [pif])
                V(lambda e: e.tensor_scalar(pif[:], pif[:], float(CH), None, ALU.mult), [pif], [pif])
                for j in range(CH):
                    V(lambda e, j=j: e.memset(jrow[:, j:j + 1], float(j + e_ * c.NPHYS * CH)), [], [jrow])
                V(lambda e: e.tensor_scalar(jrow[:], jrow[:], pif[:, 0:1], None, ALU.add), [jrow, pif], [jrow])
                V(lambda e: e.tensor_copy(idxs[:], jrow[:]), [jrow], [idxs])
                gb = [fw.sb("s_gb%d" % i, [NPG, W]) for i in range(2)]
                gi = 0
                for (csrc, gdst) in ((I["ck"], KG), (I["cv"], VG)):
                    src2 = bass.AP(csrc.t, 0, [[W, c.NE * c.NPHYS * CH], [1, W]])
                    for j in range(CH):
                        g = gb[gi % 2]; gi += 1
                        fw.swdma_gather(g, src2, idxs[:, j:j + 1], idxs)
                        fw.dma(gdst[:, j * W:(j + 1) * W], g[:], [g], [gdst], par=True)
            qn = fw.sb("s_qn", [NS, 3 * AW])
            fw.dma(qn[:], HE[T + 2:T + 2 + NS, 0:3 * AW], [HE], [qn])
            qTp = fw.sb("s_qTp", [128, AH, NQ], BF16); knT = fw.sb("s_knT", [128, AH, NS], BF16)
            vnb = fw.sb("s_vnb", [NS, AW], BF16)
            V(lambda e: e.memset(qTp[:], 0.0), [], [qTp])
            for h in range(AH):
                P(lambda e, h=h: e.transpose(pB[0][:, h * NS:(h + 1) * NS], qn[:, h * 128:(h + 1) * 128], ident[0:NS, 0:NS]), [qn, ident], [pB[0]])
                P(lambda e, h=h: e.transpose(pB[1][:, h * NS:(h + 1) * NS], qn[:, AW + h * 128:AW + (h + 1) * 128], ident[0:NS, 0:NS]), [qn, ident], [pB[1]])
            for h in range(AH):
                V(lambda e, h=h: e.tensor_copy(qTp[:, h, h * NS:(h + 1) * NS], pB[0][:, h * NS:(h + 1) * NS]), [pB[0]], [qTp])
            V(lambda e: e.tensor_copy(knT[:].rearrange("p h q -> p (h q)"), pB[1][:, 0:NQ]), [pB[1]], [knT])
            V(lambda e: e.tensor_copy(vnb[:], qn[:, 2 * AW:3 * AW]), [qn], [vnb])
            c31r = fw.sb("s_c31r", [NQ, 1]); ncmr = fw.sb("s_ncmr", [NQ, 1]); TBs = fw.sb("s_TBs", [NQ, 128]); OBt = fw.sb("s_OBt", [NQ, NS])
            for h in range(AH):
                fw.dma(c31r[h * NS:(h + 1) * NS, :], c31[0:NS, h:h + 1], [c31], [c31r], par=True)
                fw.dma(ncmr[h * NS:(h + 1) * NS, :], ncmax[0:NS, h:h + 1], [ncmax], [ncmr], par=True)
                for q in range(NS):
                    r = h * NS + q
                    fw.dma(TBs[r:r + 1, :], TZ[h, 0:1, 127 - q:255 - q], [TZ], [TBs], par=True)
                    fw.dma(OBt[r:r + 1, :], TZ[h, 0:1, 255 - q:255 - q + NS], [TZ], [OBt], par=True)
            kbuf = [fw.sb("s_kb%d" % i, [128, AW]) for i in range(2)]
            KTp = [fw.sb("s_KT%d" % i, [128, AH, 128], BF16) for i in range(2)]
            kmP = fw.sb("s_kmP", [128, AH, NPG])
            SS = fw.sb("s_SS", [NQ, PAST])
            for pg in range(NPG):
                kb = kbuf[pg % 2]; kt = KTp[pg % 2]
                fw.dma(kb[:], KG.t.ap()[pg].rearrange("(t d) -> t d", d=AW), [KG], [kb])
                for h in range(AH):
                    pt = pT[(h // 4) % 2]
                    P(lambda e, pt=pt, h=h, kb=kb: e.transpose(pt[:, (h % 4) * 128:(h % 4 + 1) * 128], kb[:, h * 128:(h + 1) * 128], ident[:]), [kb, ident], [pt])
                for g0 in range(0, AH, 4):
                    ng = min(4, AH - g0)
                    evac(kt[:, g0:g0 + ng, :], pT[(g0 // 4) % 2][:, 0:ng * 128].rearrange("p (a b) -> p a b", b=128), [pT[(g0 // 4) % 2]], [kt])
                V(lambda e, kt=kt, pg=pg: e.tensor_reduce(kmP[:, :, pg], kt[:], AX.X, ALU.add), [kt], [kmP])
                ps = pA[(pg // 4) % 2]
                for h in range(AH):
                    P(lambda e, ps=ps, h=h, kt=kt, pg=pg: e.matmul(ps[0:NQ, (pg % 4) * 128:(pg % 4 + 1) * 128], qTp[:, h, :], kt[:, h, :], start=(h == 0), stop=(h == AH - 1)), [qTp, kt], [ps])
                if pg % 4 == 3:
                    evac(SS[:, (pg - 3) * 128:(pg + 1) * 128], ps[0:NQ, :], [ps], [SS])
            kmT = fw.sb("s_kmT", [128, AH, NB], BF16)
            V(lambda e: e.tensor_tensor(kmT[:], kmP[:].rearrange("p h (n two) -> p h n two", two=2)[:, :, :, 0], kmP[:].rearrange("p h (n two) -> p h n two", two=2)[:, :, :, 1], ALU.add), [kmP], [kmT])
            for h in range(AH):
                P(lambda e, h=h: e.matmul(pB[0][0:NQ, 0:NB], qTp[:, h, :], kmT[:, h, :], start=(h == 0), stop=(h == AH - 1)), [qTp, kmT], [pB[0]])
                P(lambda e, h=h: e.matmul(pB[1][0:NQ, 0:NS], qTp[:, h, :], knT[:, h, :], start=(h == 0), stop=(h == AH - 1)), [qTp, knT], [pB[1]])
            gsb = fw.sb("s_gsb", [NQ, NB]); m8 = fw.sb("s_m8", [NQ, 8]); mbs = fw.sb("s_mb", [NQ, NB]); sm = fw.sb("s_sm", [NQ, 8])
            V(lambda e: e.tensor_copy(gsb[:], pB[0][0:NQ, 0:NB]), [pB[0]], [gsb])
            V(lambda e: e.max(m8[:], gsb[:]), [gsb], [m8])
            V(lambda e: e.tensor_scalar(mbs[:], gsb[:], m8[:, 2:3], None, ALU.is_ge), [gsb, m8], [mbs])
            V(lambda e: e.tensor_scalar(mbs[:], mbs[:], -1.0, 30000.0, ALU.add, ALU.mult), [mbs], [mbs])
            V(lambda e: e.reduce_max(sm[:, 0:1], SS[:], AX.X), [SS], [sm])
            V(lambda e: e.reduce_max(sm[:, 1:2], pB[1][0:NQ, 0:NS], AX.X), [pB[1]], [sm])
            V(lambda e: e.tensor_tensor(sm[:, 0:1], sm[:, 0:1], sm[:, 1:2], ALU.max), [sm], [sm])
            V(lambda e: e.tensor_scalar(sm[:, 2:3], sm[:, 0:1], -scale, ncmr[:, 0:1], ALU.mult, ALU.add), [sm, ncmr], [sm])
            V(lambda e: e.tensor_scalar(mbs[:], mbs[:], sm[:, 2:3], c31r[:, 0:1], ALU.add, ALU.add), [mbs, sm, c31r], [mbs])
            V(lambda e: e.scalar_tensor_tensor(SS[:].rearrange("p (n k) -> p n k", k=256), SS[:].rearrange("p (n k) -> p n k", k=256), scale,
                                                mbs[:].rearrange("p (n o) -> p n o", o=1).to_broadcast([NQ, NB, 256]), ALU.mult, ALU.add), [SS, mbs], [SS])
            V(lambda e: e.tensor_scalar(TBs[:], TBs[:], c31r[:, 0:1], None, ALU.subtract), [TBs, c31r], [TBs])
            V(lambda e: e.tensor_tensor(SS[:, PAST - 128:PAST], SS[:, PAST - 128:PAST], TBs[:], ALU.add), [SS, TBs], [SS])
            A(lambda e: e.activation(SS[:], SS[:], AF.Exp, accum_out=sm[:, 3:4]), [SS], [SS, sm])
            Lo = fw.sb("s_Lo", [NQ, NS])
            V(lambda e: e.scalar_tensor_tensor(Lo[:], pB[1][0:NQ, 0:NS], scale, OBt[:], ALU.mult, ALU.add), [pB[1], OBt], [Lo])
            A(lambda e: e.activation(Lo[:], Lo[:], AF.Exp, bias=sm[:, 2:3], scale=1.0, accum_out=sm[:, 4:5]), [Lo, sm], [Lo, sm])
            V(lambda e: e.tensor_tensor(sm[:, 5:6], sm[:, 3:4], sm[:, 4:5], ALU.add), [sm], [sm])
            V(lambda e: e.reciprocal(sm[:, 6:7], sm[:, 5:6]), [sm], [sm])
            PTs = fw.sb("s_PTs", [128, NPG, NQ], BF16); PoT = fw.sb("s_PoT", [NS, NQ], BF16)
            per = 512 // NQ
            for g0 in range(0, NPG, per):
                ng = min(per, NPG - g0)
                pt = pT[(g0 // per) % 2]
                for j in range(ng):
                    P(lambda e, pt=pt, j=j, pg=g0 + j: e.transpose(pt[:, j * NQ:(j + 1) * NQ], SS[:, pg * 128:(pg + 1) * 128], ident[0:NQ, 0:NQ]), [SS, ident], [pt])
                evac(PTs[:, g0:g0 + ng, :], pt[:, 0:ng * NQ].rearrange("p (a b) -> p a b", b=NQ), [pt], [PTs])
            P(lambda e: e.transpose(pB[0][0:NS, 0:NQ], Lo[:], ident[0:NQ, 0:NQ]), [Lo, ident], [pB[0]])
            V(lambda e: e.tensor_copy(PoT[:], pB[0][0:NS, 0:NQ]), [pB[0]], [PoT])
            vbuf = [fw.sb("s_vb%d" % i, [128, AW]) for i in range(2)]
            Vb = [fw.sb("s_Vb%d" % i, [128, AW], BF16) for i in range(2)]
            for pg in range(NPG):
                vb = vbuf[pg % 2]; vbb = Vb[pg % 2]
                fw.dma(vb[:], VG.t.ap()[pg].rearrange("(t d) -> t d", d=AW), [VG], [vb])
                A(lambda e, vb=vb, vbb=vbb: e.copy(vbb[:], vb[:]), [vb], [vbb])
                for b in range(NAB):
                    w = min(512, AW - b * 512)
                    P(lambda e, b=b, w=w, pg=pg, vbb=vbb: e.matmul(pA[2 + b][0:NQ, 0:w], PTs[:, pg, :], vbb[:, b * 512:b * 512 + w], start=(pg == 0), stop=False), [PTs, vbb], [pA[2 + b]])
            OVs = fw.sb("s_OVs", [NQ, AW])
            for b in range(NAB):
                w = min(512, AW - b * 512)
                P(lambda e, b=b, w=w: e.matmul(pA[2 + b][0:NQ, 0:w], PoT[:], vnb[:, b * 512:b * 512 + w], start=False, stop=True), [PoT, vnb], [pA[2 + b]])
                V(lambda e, b=b, w=w: e.tensor_scalar(OVs[:, b * 512:b * 512 + w], pA[2 + b][0:NQ, 0:w], sm[:, 6:7], None, ALU.mult), [pA[2 + b], sm], [OVs])
            for h in range(AH):
                fw.dma(Y[T:T + NS, h * 128:(h + 1) * 128], OVs[h * NS:(h + 1) * NS, h * 128:(h + 1) * 128], [OVs], [Y], par=True)

    ctx = dict(locals())
    return ctx


def build_full(c, with_cache=False):
    ctx = build(c)
    fw = ctx["fw"]; I = ctx["I"]; O = ctx["O"]; HE = ctx["HE"]; Y = ctx["Y"]
    T, NS, AW, D = c.T, c.NS, c.AW, c.D
    zero = ctx["zero"]
    for j in range(0, D, 512):
        for r0 in range(0, c.R, 128):
            fw.dma(Y[r0:r0 + 128, j:j + 512], zero[:, :], [zero], [Y], par=True)
    fw.barrier()
    for l in range(c.L):
        if l % 2 == 0:
            e = l // 2
            xb0 = 3 * AW
            fw.dma(HE[T + 1:T + 2, xb0:xb0 + c.BCOLS], I["ssh"][e:e + 1, :], [], [HE], par=True)
            ctx["gemm_f1"](I["w_in_e"].t.ap()[e], c.COLS_E, HE, ctx["herow"])
            fw.dma(O["nkp"][e, :, :], HE[1:1 + T, AW:2 * AW], [HE], [O["nkp"]], par=True, is_out=True)
            fw.dma(O["nvp"][e, :, :], HE[1:1 + T, 2 * AW:3 * AW], [HE], [O["nvp"]], par=True, is_out=True)
            fw.dma(O["nks"][e, :, :], HE[T + 2:T + 2 + NS, AW:2 * AW], [HE], [O["nks"]], par=True, is_out=True)
            fw.dma(O["nvs"][e, :, :], HE[T + 2:T + 2 + NS, 2 * AW:3 * AW], [HE], [O["nvs"]], par=True, is_out=True)
            fw.dma(O["nshp"][e:e + 1, :], HE[T:T + 1, xb0:xb0 + c.BCOLS], [HE], [O["nshp"]], par=True, is_out=True)
            fw.dma(O["nshs"][e:e + 1, :], HE[T + 1 + NS:T + 2 + NS, xb0:xb0 + c.BCOLS], [HE], [O["nshs"]], par=True, is_out=True)
            for st in ("moba_prompt_stage", "moba_sample_stage", "rwkv_stage"):
                if st in ctx:
                    ctx[st](e)
            ctx["gate_stage_even"](e)
        else:
            ctx["odd_stage"](l // 2)
        ctx["xattn_stage"](l, l == c.L - 1)
    fw.finish()
    return fw, ctx


_IN_MAP = dict(
    w_in_e="w_in_even", w_out_e="w_out_even", mu="rwkv_mu", w0="rwkv_w0", w_up="rwkv_w_up", a0="rwkv_a0", a_up="rwkv_a_up",
    k_k="rwkv_k_k", k_a="rwkv_k_a", gn_g="rwkv_gn_g", gn_b="rwkv_gn_b", t5="t5_bias", w_in_o="w_in_odd", pool_w="pool_w",
    pool_scale="pool_scale", w_out_o="w_out_odd", wq="xattn_w_q", wk="xattn_w_k", wv="xattn_w_v", wo="xattn_w_o",
    lnm_g="ln_mix_g", lnm_b="ln_mix_b", lnx_g="ln_x_g", lnx_b="ln_x_b")


def kernel(**inp):
    from concourse.bass_utils import run_bass_kernel_spmd
    c = Cfg()
    fw, ctx = build_full(c)
    used = set(ctx["I"].keys())
    hc = host_consts(c)
    A = lambda k: np.ascontiguousarray(np.asarray(inp[k]))
    shared = {}
    for k, src in _IN_MAP.items():
        shared[k] = A(src)
    shared["r_k"] = A("rwkv_r_k").reshape(c.NE, c.BW)
    for k, v in hc.items():
        shared[k] = v
    in_maps = []
    for core in range(8):
        b = core % 4
        m = dict(shared)
        m["xp"] = A("x_prompt")[b]
        m["xs"] = A("x_sample")[core]
        m["ptab"] = A("page_table")[core:core + 1].astype(np.int32)
        m["srw"] = A("state_rwkv")[:, core].reshape(c.NE, c.BH * 64, 64)
        m["ssh"] = A("state_shift")[:, core]
        m["spool"] = A("state_pool")[:, core]
        m["cmk"] = A("cache_mem_k")[:, core].reshape(c.L, c.ML, c.D)
        m["cmv"] = A("cache_mem_v")[:, core].reshape(c.L, c.ML, c.D)
        m["memp"] = A("mem_prompt")[b]
        if "ck" in used:
            m["ck"] = A("cache_moba_k").reshape(c.NE, c.NPHYS * 128, c.AW)
            m["cv"] = A("cache_moba_v").reshape(c.NE, c.NPHYS * 128, c.AW)
        in_maps.append({k: np.ascontiguousarray(v) for k, v in m.items() if k in used})
    res = run_bass_kernel_spmd(fw.nc, in_maps, core_ids=list(range(8)))
    R = res.results
    st = lambda name, cores: np.stack([R[i][name] for i in cores], 0)
    P4 = range(4); S8 = range(8)
    NE, NO, L, T, NS, D = c.NE, c.NO, c.L, c.T, c.NS, c.D
    y_prompt = st("yp", P4)
    y_sample = st("ys", S8)
    nkp = st("nkp", P4).transpose(1, 0, 2, 3).reshape(NE, 4, T, c.AH, 128)
    nvp = st("nvp", P4).transpose(1, 0, 2, 3).reshape(NE, 4, T, c.AH, 128)
    nrp = st("nrp", P4).transpose(1, 0, 2, 3).reshape(NE, 4, c.BH, 64, 64)
    nshp = st("nshp", P4).transpose(1, 0, 2)
    npp = st("npp", P4).transpose(1, 0, 2, 3)
    nmk = st("nmk", P4).transpose(1, 0, 2, 3).reshape(L, 4, c.ML, c.XH, c.XHD)
    nmv = st("nmv", P4).transpose(1, 0, 2, 3).reshape(L, 4, c.ML, c.XH, c.XHD)
    nks = st("nks", S8).transpose(1, 0, 2, 3).reshape(NE, 8, NS, c.AH, 128)
    nvs = st("nvs", S8).transpose(1, 0, 2, 3).reshape(NE, 8, NS, c.AH, 128)
    nrs = st("nrs", S8).transpose(1, 0, 2, 3).reshape(NE, 8, c.BH, 64, 64)
    nshs = st("nshs", S8).transpose(1, 0, 2)
    nps = st("nps", S8).transpose(1, 0, 2, 3)
    outs = (y_prompt, y_sample, nkp, nvp, nrp, nshp, npp, nmk, nmv, nks, nvs, nrs, nshs, nps)
    return tuple(np.ascontiguousarray(o.astype(np.float32)) for o in outs)
```
